# Optimizing a Trainium2 kernel written in Bass

```python
import jax, jax.numpy as jnp
from jax import lax
import numpy as np

D_MODEL = 2048
BATCH = 4
SEQ = 4096
DEPTH = 2

CHUNK = 64
Q_BLOCK = 128
MEM_LEN = 256
N_A_LAYERS = DEPTH // 2
N_B_LAYERS = DEPTH - N_A_LAYERS

MIX_WIDTH = D_MODEL
MEM_HEADS = 4
MEM_HEAD_DIM = D_MODEL // 16
MEM_WIDTH = MEM_HEADS * MEM_HEAD_DIM
TOK_WIDTH = MIX_WIDTH - MEM_WIDTH

LRU_WIDTH = TOK_WIDTH
LRU_BLOCKS = 12
LRU_BLOCK_DIM = LRU_WIDTH // LRU_BLOCKS
CONV_WIDTH = 4
LRU_C = 8.0

V_HEAD_DIM = 128
MLA_HEADS = TOK_WIDTH // V_HEAD_DIM
QK_NOPE_DIM = 128
QK_ROPE_DIM = 64
QK_HEAD_DIM = QK_NOPE_DIM + QK_ROPE_DIM
Q_LORA_RANK = 768
KV_LORA_RANK = 512
ROPE_THETA = 10000.0

A_IN_WIDTH = 2 * LRU_WIDTH + MEM_WIDTH
B_IN_WIDTH = Q_LORA_RANK + MEM_WIDTH

D_FF = 4 * D_MODEL
EPS = 1e-6

kernel_name = "yoco_rglru_mla_memory_trunk"


def rms_norm(x, g):
    xf = x.astype(jnp.float32)
    xf = xf * lax.rsqrt(jnp.mean(xf * xf, axis=-1, keepdims=True) + EPS)
    return xf.astype(x.dtype) * g


def rope_tables(positions):
    half = QK_ROPE_DIM // 2
    inv_freq = ROPE_THETA ** (-jnp.arange(half, dtype=jnp.float32) / half)
    ang = positions.astype(jnp.float32)[..., None] * inv_freq
    return jnp.cos(ang), jnp.sin(ang)


def apply_rope(x, cos, sin):
    half = QK_ROPE_DIM // 2
    x1, x2 = x[..., :half], x[..., half:]
    c = cos.astype(x.dtype)
    s = sin.astype(x.dtype)
    return jnp.concatenate([x1 * c - x2 * s, x2 * c + x1 * s], axis=-1)


def rglru_mixer(xb, gate, conv_w, conv_b, w_r, b_r, w_i, b_i, lam):
    B, S, C = xb.shape
    xc = lax.conv_general_dilated(
        xb, conv_w[:, None, :], window_strides=(1,), padding=[(CONV_WIDTH - 1, 0)],
        dimension_numbers=('NWC', 'WIO', 'NWC'), feature_group_count=C) + conv_b
    xh = xc.reshape(B, S, LRU_BLOCKS, LRU_BLOCK_DIM)
    r = jax.nn.sigmoid(jnp.einsum('bsnd,nde->bsne', xh, w_r).reshape(B, S, C) + b_r)
    i = jax.nn.sigmoid(jnp.einsum('bsnd,nde->bsne', xh, w_i).reshape(B, S, C) + b_i)
    log_a = (-LRU_C * r.astype(jnp.float32)) * jax.nn.softplus(-lam.astype(jnp.float32))
    a = jnp.exp(log_a)
    b = jnp.sqrt(-jnp.expm1(2.0 * log_a)) * (i * xc).astype(jnp.float32)

    def combine(left, right):
        return (left[0] * right[0], right[0] * left[1] + right[1])

    _, h = lax.associative_scan(combine, (a, b), axis=1)
    return h.astype(xb.dtype) * jax.nn.gelu(gate)


def shared_mla_kv(h, g_in, w_down, g_latent, w_up, cos, sin):
    B, S, _ = h.shape
    ckv = rms_norm(h, g_in) @ w_down
    latent = rms_norm(ckv[..., :KV_LORA_RANK], g_latent)
    k_pe = apply_rope(ckv[..., KV_LORA_RANK:], cos, sin)
    kv = (latent @ w_up).reshape(B, S, MLA_HEADS, QK_NOPE_DIM + V_HEAD_DIM)
    return kv[..., :QK_NOPE_DIM], k_pe, kv[..., QK_NOPE_DIM:]


def mla_block_causal(q_nope, q_pe, k_nope, k_pe, v):
    B, S, H, _ = q_nope.shape
    scale = QK_HEAD_DIM ** -0.5
    outs = []
    for blk in range(S // Q_BLOCK):
        q0 = blk * Q_BLOCK
        kl = q0 + Q_BLOCK
        s = (jnp.einsum('bqhd,bkhd->bhqk', q_nope[:, q0:kl], k_nope[:, :kl])
             + jnp.einsum('bqhd,bkd->bhqk', q_pe[:, q0:kl], k_pe[:, :kl])).astype(jnp.float32) * scale
        q_chunk = (q0 + jnp.arange(Q_BLOCK)) // CHUNK
        k_chunk = jnp.arange(kl) // CHUNK
        mask = k_chunk[None, :] <= q_chunk[:, None]
        p = jax.nn.softmax(jnp.where(mask, s, -1e30), axis=-1).astype(v.dtype)
        outs.append(jnp.einsum('bhqk,bkhd->bqhd', p, v[:, :kl]))
    return jnp.concatenate(outs, axis=1).reshape(B, S, H * V_HEAD_DIM)


def memory_attention(qm, mk, mv):
    B, S, _ = qm.shape
    q = qm.reshape(B, S, MEM_HEADS, MEM_HEAD_DIM)
    s = jnp.einsum('bshd,bmhd->bhsm', q, mk).astype(jnp.float32) * (MEM_HEAD_DIM ** -0.5)
    p = jax.nn.softmax(s, axis=-1).astype(mv.dtype)
    return jnp.einsum('bhsm,bmhd->bshd', p, mv).reshape(B, S, MEM_WIDTH)


def setup_inputs(seed: int = 0) -> dict:
    key = jax.random.key(seed)
    ks = iter(jax.random.split(key, 40))
    nrm = lambda shape, scale: jax.random.normal(next(ks), shape, jnp.float32) * scale
    gain = lambda shape: 1.0 + 0.05 * jax.random.normal(next(ks), shape, jnp.float32)

    x = jax.random.normal(next(ks), (BATCH, SEQ, D_MODEL), jnp.float32)
    mem = jax.random.normal(next(ks), (BATCH, MEM_LEN, D_MODEL), jnp.float32)
    offset = jax.random.randint(next(ks), (BATCH, 1), 0, 64) * CHUNK
    positions = (offset + jnp.arange(SEQ)[None, :]).astype(jnp.int32)

    u = jax.random.uniform(next(ks), (N_A_LAYERS, LRU_WIDTH), jnp.float32, minval=0.9, maxval=0.999)
    a0 = u ** (1.0 / LRU_C)
    a_lambda = jnp.log(a0) - jnp.log1p(-a0)

    return {
        "x": x, "mem": mem, "positions": positions,
        "g_mix_pre": gain((DEPTH, D_MODEL)),
        "g_mix_post": gain((DEPTH, D_MODEL)),
        "g_mlp_pre": gain((DEPTH, D_MODEL)),
        "g_mlp_post": gain((DEPTH, D_MODEL)),
        "g_mem": gain((DEPTH, D_MODEL)),
        "w_mem_k": nrm((DEPTH, D_MODEL, MEM_WIDTH), D_MODEL ** -0.5),
        "w_mem_v": nrm((DEPTH, D_MODEL, MEM_WIDTH), D_MODEL ** -0.5),
        "w_o": nrm((DEPTH, MIX_WIDTH, D_MODEL), MIX_WIDTH ** -0.5),
        "w_ff1": nrm((DEPTH, D_MODEL, D_FF), D_MODEL ** -0.5),
        "w_ff2": nrm((DEPTH, D_FF, D_MODEL), D_FF ** -0.5),
        "a_w_in": nrm((N_A_LAYERS, D_MODEL, A_IN_WIDTH), D_MODEL ** -0.5),
        "a_conv_w": nrm((N_A_LAYERS, CONV_WIDTH, LRU_WIDTH), CONV_WIDTH ** -0.5),
        "a_conv_b": nrm((N_A_LAYERS, LRU_WIDTH), 0.01),
        "a_w_rgate": nrm((N_A_LAYERS, LRU_BLOCKS, LRU_BLOCK_DIM, LRU_BLOCK_DIM), LRU_BLOCK_DIM ** -0.5),
        "a_b_rgate": nrm((N_A_LAYERS, LRU_WIDTH), 0.01),
        "a_w_igate": nrm((N_A_LAYERS, LRU_BLOCKS, LRU_BLOCK_DIM, LRU_BLOCK_DIM), LRU_BLOCK_DIM ** -0.5),
        "a_b_igate": nrm((N_A_LAYERS, LRU_WIDTH), 0.01),
        "a_lambda": a_lambda,
        "b_w_in": nrm((N_B_LAYERS, D_MODEL, B_IN_WIDTH), D_MODEL ** -0.5),
        "b_g_qa": gain((N_B_LAYERS, Q_LORA_RANK)),
        "b_w_qb": nrm((N_B_LAYERS, Q_LORA_RANK, MLA_HEADS * QK_HEAD_DIM), Q_LORA_RANK ** -0.5),
        "kv_g_in": gain((D_MODEL,)),
        "kv_w_down": nrm((D_MODEL, KV_LORA_RANK + QK_ROPE_DIM), D_MODEL ** -0.5),
        "kv_g_latent": gain((KV_LORA_RANK,)),
        "kv_w_up": nrm((KV_LORA_RANK, MLA_HEADS * (QK_NOPE_DIM + V_HEAD_DIM)), KV_LORA_RANK ** -0.5),
    }


def reference(x, mem, positions,
              g_mix_pre, g_mix_post, g_mlp_pre, g_mlp_post, g_mem, w_mem_k, w_mem_v, w_o, w_ff1, w_ff2,
              a_w_in, a_conv_w, a_conv_b, a_w_rgate, a_b_rgate, a_w_igate, a_b_igate, a_lambda,
              b_w_in, b_g_qa, b_w_qb,
              kv_g_in, kv_w_down, kv_g_latent, kv_w_up):
    B, S, _ = x.shape
    cos, sin = rope_tables(positions)
    h = x
    shared = None
    for layer in range(DEPTH):
        hn = rms_norm(h, g_mix_pre[layer])
        mn = rms_norm(mem, g_mem[layer])
        mk = (mn @ w_mem_k[layer]).reshape(B, MEM_LEN, MEM_HEADS, MEM_HEAD_DIM)
        mv = (mn @ w_mem_v[layer]).reshape(B, MEM_LEN, MEM_HEADS, MEM_HEAD_DIM)
        if layer < N_A_LAYERS:
            la = layer
            proj = hn @ a_w_in[la]
            xb = proj[..., :LRU_WIDTH]
            gate = proj[..., LRU_WIDTH:2 * LRU_WIDTH]
            qm = proj[..., 2 * LRU_WIDTH:]
            tok = rglru_mixer(xb, gate, a_conv_w[la], a_conv_b[la], a_w_rgate[la], a_b_rgate[la],
                              a_w_igate[la], a_b_igate[la], a_lambda[la])
        else:
            lb = layer - N_A_LAYERS
            if shared is None:
                shared = shared_mla_kv(h, kv_g_in, kv_w_down, kv_g_latent, kv_w_up, cos, sin)
            k_nope, k_pe, v = shared
            proj = hn @ b_w_in[lb]
            cq = rms_norm(proj[..., :Q_LORA_RANK], b_g_qa[lb])
            qm = proj[..., Q_LORA_RANK:]
            q = (cq @ b_w_qb[lb]).reshape(B, S, MLA_HEADS, QK_HEAD_DIM)
            q_nope = q[..., :QK_NOPE_DIM]
            q_pe = apply_rope(q[..., QK_NOPE_DIM:], cos[:, :, None, :], sin[:, :, None, :])
            tok = mla_block_causal(q_nope, q_pe, k_nope, k_pe, v)
        mem_out = memory_attention(qm, mk, mv)
        y = jnp.concatenate([tok, mem_out], axis=-1) @ w_o[layer]
        h = h + rms_norm(y, g_mix_post[layer])
        f = rms_norm(h, g_mlp_pre[layer]) @ w_ff1[layer]
        f = jnp.square(jax.nn.relu(f)) @ w_ff2[layer]
        h = h + rms_norm(f, g_mlp_post[layer])
    return h
```

```python
import math
import numpy as np
import concourse.bass as bass
import concourse.mybir as mybir
from concourse.bass_utils import run_bass_kernel_spmd

F32 = mybir.dt.float32
BF16 = mybir.dt.bfloat16
I32 = mybir.dt.int32
AF = mybir.ActivationFunctionType
ALU = mybir.AluOpType
AX = mybir.AxisListType

D = 2048
KC = 16
S_CORE = 2048
T = 512
NT = S_CORE // T
DFF = 8192
LRU = 1536
NCH = 12
MEMW = 512
MEML = 256
QL = 768
KVL = 512
ROPE = 64
NH = 12
EPS = 1e-6
SLOT = 4096
NSLOT = 4
TWO_PI = 2.0 * math.pi

ENGS = ("pe", "act", "dve", "pool", "sp")


class Buf:
    __slots__ = ("name", "lw", "rd", "wsem", "rsem")

    def __init__(self, name):
        self.name = name
        self.lw = None
        self.rd = {}
        self.wsem = None
        self.rsem = None


class Prog:
    def __init__(self, nc):
        self.nc = nc
        self.ops = {e: [] for e in ENGS}
        self.cnt = {e: 0 for e in ENGS}
        self.seen = {e: {} for e in ENGS}
        self.esem = {}
        self.dsems = []

    def _need(self, eng, dep, waits):
        if dep is None:
            return
        kind, key, val = dep
        if kind == "e" and key == "pe" and eng == "pe":
            return
        k = (kind, key if kind == "e" else id(key))
        if self.seen[eng].get(k, 0) >= val:
            return
        self.seen[eng][k] = val
        waits.append((kind, key, val))

    def _deps(self, eng, reads, writes):
        waits = []
        for b in reads:
            self._need(eng, b.lw, waits)
        for b in writes:
            self._need(eng, b.lw, waits)
            for d in b.rd.values():
                self._need(eng, d, waits)
        return waits

    def op(self, eng, fn, reads=(), writes=()):
        waits = self._deps(eng, reads, writes)
        self.cnt[eng] += 1
        dep = ("e", eng, self.cnt[eng])
        self.ops[eng].append((waits, fn, ("e", None, 1)))
        for b in reads:
            b.rd[eng] = dep
        for b in writes:
            b.lw = dep
            b.rd = {}
        return dep

    def dma(self, queue, fn, reads=(), writes=(), sem_buf=None, kind="w", inc=16):
        waits = self._deps(queue, reads, writes)
        if sem_buf is None:
            sem_buf = writes[0] if kind == "w" else reads[0]
        if kind == "w":
            if sem_buf.wsem is None:
                sem_buf.wsem = [None, 0]
                self.dsems.append(sem_buf.wsem)
            rec = sem_buf.wsem
        else:
            if sem_buf.rsem is None:
                sem_buf.rsem = [None, 0]
                self.dsems.append(sem_buf.rsem)
            rec = sem_buf.rsem
        rec[1] += inc
        dep = ("d", rec, rec[1])
        self.ops[queue].append((waits, fn, ("d", rec, inc)))
        for b in reads:
            b.rd[id(rec)] = dep
        for b in writes:
            b.lw = dep
            b.rd = {}
        return dep

    def wait_all(self, eng, bufs):
        waits = []
        for b in bufs:
            self._need(eng, b.lw, waits)
            for d in b.rd.values():
                self._need(eng, d, waits)
        self.ops[eng].append((waits, None, None))

    def barrier(self, final=False):
        skip = getattr(self, "bg_sems", ())
        snap = [("e", e, self.cnt[e]) for e in ENGS if self.cnt[e] > 0 and (final or e not in ("pool", "sp"))]
        snap += [("d", rec, rec[1]) for rec in self.dsems if rec[1] > 0 and (final or not any(rec is x for x in skip))]
        for e in ENGS:
            if e in ("pool", "sp") and not final:
                continue
            waits = []
            for d in snap:
                if d[0] == "e" and d[1] == e:
                    continue
                self._need(e, d, waits)
            self.ops[e].append((waits, None, None))

    def emit(self):
        nc = self.nc
        for e in ENGS:
            self.esem[e] = nc.alloc_semaphore("es_" + e)
        for i, rec in enumerate(self.dsems):
            rec[0] = nc.alloc_semaphore("ds_%d" % i)
        ops = self.ops
        esem = self.esem

        def run(e, eo):
            for waits, fn, inc in ops[e]:
                for kind, key, val in waits:
                    if kind == "e":
                        eo.wait_ge(esem[key], val)
                    else:
                        eo.wait_ge(key[0], val)
                if fn is None:
                    continue
                ins = fn(eo)
                if inc[0] == "e":
                    ins.then_inc(esem[e], 1)
                else:
                    ins.then_inc(inc[1][0], inc[2])

        with nc.Block() as block:
            @block.sync
            def _(eo):
                run("sp", eo)

            @block.scalar
            def _(eo):
                run("act", eo)

            @block.vector
            def _(eo):
                run("dve", eo)

            @block.gpsimd
            def _(eo):
                run("pool", eo)

            @block.tensor
            def _(eo):
                run("pe", eo)


WSPECS = {
    "w_mem_k0": (D, MEMW, 512), "w_mem_v0": (D, MEMW, 512),
    "a_w_in": (D, 3584, 512),
    "w_o0": (D, D, 512), "w_ff1_0": (D, DFF, 512), "w_ff2_0": (DFF, D, 512),
    "kv_w_down": (D, KVL + ROPE, 576),
    "w_mem_k1": (D, MEMW, 512), "w_mem_v1": (D, MEMW, 512),
    "b_w_in": (D, 1280, 256), "b_w_qb": (QL, NH * 192, 192), "kv_w_up": (KVL, NH * 256, 256),
    "w_o1": (D, D, 512), "w_ff1_1": (D, DFF, 512), "w_ff2_1": (DFF, D, 512),
}
L0_W = ["w_mem_k0", "w_mem_v0", "a_w_in", "w_o0", "w_ff1_0", "w_ff2_0", "kv_w_down"]
L1_W = ["w_mem_k1", "w_mem_v1", "b_w_in", "b_w_qb", "kv_w_up", "w_o1", "w_ff1_1", "w_ff2_1"]


class _Stop(Exception):
    pass


def build(mode, stop=None, dbg=None):
    try:
        return _build(mode, stop, dbg)
    except _Stop as s:
        return s.args[0]


def _build(mode, stop, dbg):
    from contextlib import ExitStack
    nc = bass.Bass("TRN2", target_bir_lowering=False)
    P = Prog(nc)
    uniq = [0]
    doA = mode in ("A", "F")
    doB = mode in ("B", "F")
    L = [0 if doA else 1]

    def din(name, shape, dt=F32):
        return nc.dram_tensor(name, list(shape), dt, kind="ExternalInput").ap()

    def dout(name, shape, dt=F32):
        return nc.dram_tensor(name, list(shape), dt, kind="ExternalOutput").ap()

    def dscr(name, shape, dt=F32):
        return nc.dram_tensor(name, list(shape), dt).ap()

    ident_d = din("ident", [128, 128])
    flag_d = din("flag", [128, 1])
    invf_d = din("invf", [64, 1])
    pos_d = din("pos64", [64, S_CORE], I32)
    mem_d = din("mem", [MEML, D])
    wnames = (L0_W if doA else []) + (L1_W if doB else [])
    wsrc = {n: din(n, [WSPECS[n][0], WSPECS[n][1]]) for n in wnames}
    vec = {}

    def vin(name, shape):
        vec[name] = din(name, shape)

    for l_ in ([0] if doA else []) + ([1] if doB else []):
        vin("gc_mix_pre%d" % l_, [128, KC]); vin("gc_mlp_pre%d" % l_, [128, KC]); vin("gc_mem%d" % l_, [128, KC])
        vin("gr_mix_post%d" % l_, [1, D]); vin("gr_mlp_post%d" % l_, [1, D])
    NRB = S_CORE // 128
    if doA:
        x_own = din("x_own", [S_CORE, D])
        x_prev = din("x_prev", [S_CORE, D])
        vin("conv_w", [128, NCH * 4]); vin("conv_b", [128, NCH]); vin("b_r", [128, NCH]); vin("b_i", [128, NCH])
        vin("lam", [128, NCH]); vin("gc_kv_in", [128, KC]); vin("gr_kv_lat", [1, KVL])
        wr_d = din("a_w_r", [128, NCH * 128]); wi_d = din("a_w_i", [128, NCH * 128])
    KVR = 4 * 128 + ROPE
    if mode == "A":
        h1 = dout("h1", [S_CORE, D])
        kvx_own = dout("kvx_own", [NT, KVR, T], BF16)
    elif mode == "B":
        h1 = din("h1", [S_CORE, D])
        kvx_own = din("kvx_own", [NT, KVR, T], BF16)
        kv_all = din("kv_all", [NT, 2 * KVR, T], BF16)
    else:
        h1 = dscr("h1", [S_CORE, D])
        kvx_own = dscr("kvx_own", [NT, KVR, T], BF16)
        kv_all = dscr("kv_all", [NT, 2 * KVR, T], BF16)
    B_kv_all = [Buf("kv_all%d" % i) for i in range(NT)]
    B_kvx = [Buf("kvx%d" % i) for i in range(NT)]
    if doB:
        vin("gc_qa", [128, 6])
        out_d = dout("out", [S_CORE, D])
        hmid = dscr("hmid", [S_CORE, D])
        mixd = dscr("mixd", [16, 128, S_CORE], BF16)
        B_hmid = [Buf("hmid%d" % i) for i in range(NRB)]
        B_outr = [Buf("out%d" % i) for i in range(NRB)]
        B_mixd = [Buf("mixd%d" % i) for i in range(16)]
    B_h1 = [Buf("h1_%d" % i) for i in range(NRB)]
    B_lat_own = Buf("lat_own"); B_kpe_own = Buf("kpe_own")
    B_in = Buf("inputs")
    dbg_outs = {}

    def stage(name, dumps=()):
        if stop != name:
            return
        for dn, ap, bufs in dumps:
            o = dout("dbg_" + dn, list(ap.shape), ap.dtype)
            P.dma("sp", lambda e, o=o, ap=ap: e.dma_start(out=o, in_=ap), reads=bufs, writes=[Buf("dbg")], kind="r",
                  sem_buf=Buf("dbgs"))
        P.barrier(final=True)
        P.emit()
        raise _Stop(nc)


    wb = {}
    B_wb = {}
    P.bg_sems = []
    for n in wnames:
        Kd, Nd, gw = WSPECS[n]
        wb[n] = dscr("wb_" + n, [Nd // gw, 128, (Kd // 128) * gw], BF16)
        B_wb[n] = Buf("wb_" + n)

    def precast(names, gate=()):
        for n in names:
            Kd, Nd, gw = WSPECS[n]
            kcn = Kd // 128
            for g in range(Nd // gw):
                src = wsrc[n][:, g * gw:(g + 1) * gw].rearrange("(kc p) c -> p kc c", p=128)
                dst = wb[n][g].rearrange("p (kc c) -> p kc c", c=gw)
                nsp = 2 if kcn > 32 else 1
                hk = kcn // nsp
                for hh in range(nsp):
                    P.dma("pool", lambda e, s=src[:, hh * hk:(hh + 1) * hk, :], d=dst[:, hh * hk:(hh + 1) * hk, :]:
                          e.dma_start(out=d, in_=s), reads=list(gate), writes=[B_wb[n]])
            if not any(B_wb[n].wsem is x for x in P.bg_sems):
                P.bg_sems.append(B_wb[n].wsem)

    if doA:
        precast(["a_w_in", "w_mem_k0", "w_mem_v0", "w_o0"])
    else:
        precast(L1_W)

    def sb(name, shape, dt=F32):
        return nc.alloc_sbuf_tensor("s_" + name, list(shape), dt)

    wring = [sb("wring%d" % i, [128, SLOT], BF16) for i in range(NSLOT)]
    B_wring = [Buf("wring%d" % i) for i in range(NSLOT)]
    ring_i = [0]
    ident_f = sb("ident_f", [128, 128]); ident = sb("ident", [128, 128], BF16)
    ones_bf = sb("ones_bf", [128, 128], BF16)
    flag = sb("flag", [128, 1]); flagones = sb("flagones", [128, 128], BF16)
    zbias = sb("zbias", [128, 1])
    B_const = Buf("const")
    P.dma("sp", lambda e: e.dma_start(out=ident_f[:], in_=ident_d), writes=[B_const])
    P.dma("sp", lambda e: e.dma_start(out=flag[:], in_=flag_d), writes=[B_const])
    svec = {}
    for n, ap in vec.items():
        if n.startswith("gr_"):
            continue
        svec[n] = sb("v_" + n, list(ap.shape))
        P.dma("sp", lambda e, o=svec[n], i=ap: e.dma_start(out=o[:], in_=i), writes=[B_const])
    posf = sb("posf", [64, S_CORE]); invf = sb("invf", [64, 1])
    with nc.sbuf_tensor("s_posi", [64, S_CORE], I32) as posi:
        P.dma("sp", lambda e: e.dma_start(out=posi[:], in_=pos_d), writes=[B_const])
        P.dma("sp", lambda e: e.dma_start(out=invf[:], in_=invf_d), writes=[B_const])
        P.op("dve", lambda e: e.tensor_copy(out=posf[:], in_=posi[:]), [B_const], [B_const])
        P.op("dve", lambda e: e.tensor_copy(out=ident[:], in_=ident_f[:]), [B_const], [B_const])
        P.op("dve", lambda e: e.memset(ones_bf[:], 1.0), [B_const], [B_const])
        P.op("dve", lambda e: e.memset(zbias[:], 0.0), [B_const], [B_const])
        P.op("dve", lambda e: e.tensor_scalar(out=flagones[:], in0=ones_bf[:], scalar1=flag[:, 0:1], scalar2=None, op0=ALU.mult),
             [B_const], [B_const])
        P.barrier()
    stage("init", [("posf", posf[:], [B_const]), ("flagones", flagones[:], [B_const])])
    grow = sb("grow", [128, D]); B_grow = Buf("grow")
    hs = [sb("hs%d" % i, [128, D]) for i in range(2)]; B_hs = [Buf("hs%d" % i) for i in range(2)]
    hs_i = [0]
    xn = sb("xn", [128, D], BF16); B_xn = Buf("xn")
    junk = sb("junk", [128, 512], BF16)
    stat = sb("stat", [128, 64]); B_stat = Buf("stat")
    stat_i = [0]
    ssqA = sb("ssqA", [128, 16]); ssqB = sb("ssqB", [128, 16])
    mkT = sb("mkT", [128, 4, MEML], BF16); B_mkT = Buf("mkT")
    mvt = sb("mvt", [128, 2, MEMW], BF16); B_mvt = Buf("mvt")
    pT = [sb("pT%d" % i, [128, T], BF16) for i in range(4)]; B_pT = [Buf("pT%d" % i) for i in range(4)]
    pT_i = [0]
    rec = sb("rec", [128, T]); B_rec = Buf("rec")
    ps = [nc.alloc_psum_tensor("ps%d" % i, [128, 512], F32) for i in range(8)]
    B_ps = [Buf("ps%d" % i) for i in range(8)]
    ps_i = [0]

    def next_ps():
        i = ps_i[0] % 8
        ps_i[0] += 1
        return ps[i], B_ps[i]

    def next_stat(n=1):
        if stat_i[0] + n > 64:
            stat_i[0] = 0
        a = stat[:, stat_i[0]:stat_i[0] + n]
        stat_i[0] += n
        return a

    def act(out, in_, func, reads, writes, scale=1.0, bias=0.0, accum_out=None):
        if accum_out is None:
            P.op("act", lambda e: e.activation(out=out, in_=in_, func=func, bias=bias, scale=scale), reads, writes)
        else:
            P.op("act", lambda e: e.activation(out=out, in_=in_, func=func, bias=bias, scale=scale, accum_out=accum_out),
                 reads, writes)

    def tt(eng, out, in0, in1, op, reads, writes):
        P.op(eng, lambda e: e.tensor_tensor(out=out, in0=in0, in1=in1, op=op), reads, writes)

    def ts(eng, out, in0, s1, s2, op0, op1, reads, writes):
        if s2 is None:
            P.op(eng, lambda e: e.tensor_scalar(out=out, in0=in0, scalar1=s1, scalar2=None, op0=op0), reads, writes)
        else:
            P.op(eng, lambda e: e.tensor_scalar(out=out, in0=in0, scalar1=s1, scalar2=s2, op0=op0, op1=op1), reads, writes)

    def stt(out, in0, scalar, in1, op0, op1, reads, writes):
        P.op("dve", lambda e: e.scalar_tensor_tensor(out=out, in0=in0, scalar=scalar, in1=in1, op0=op0, op1=op1),
             reads, writes)

    def cp(eng, out, in_, reads, writes):
        P.op(eng, lambda e: e.tensor_copy(out=out, in_=in_), reads, writes)

    def mm(out, lhsT, rhs, start, stop, reads, writes):
        P.op("pe", lambda e: e.matmul(out, lhsT=lhsT, rhs=rhs, start=start, stop=stop), reads, writes)

    def piece(n, g, k0, nk):
        Kd, Nd, gw = WSPECS[n]
        i = ring_i[0] % NSLOT
        ring_i[0] += 1
        dst = wring[i][:, 0:nk * gw].rearrange("p (k c) -> p k c", c=gw)
        src = wb[n][g].rearrange("p (k c) -> p k c", c=gw)[:, k0:k0 + nk, :]
        P.dma("sp", lambda e: e.dma_start(out=dst, in_=src), reads=[B_wb[n]], writes=[B_wring[i]])
        return dst, B_wring[i]

    def rsqrt_ap(dst, src, n_feat, reads, writes, B_tmp=None):
        ts("dve", dst, src, 1.0 / n_feat, EPS, ALU.mult, ALU.add, reads, writes)
        act(dst, dst, AF.Ln, writes, writes)
        act(dst, dst, AF.Exp, writes, writes, scale=-0.5)

    def load_rows(src_ap, src_bufs, queue="act"):
        i = hs_i[0] % 2
        hs_i[0] += 1
        P.dma(queue, lambda e: e.dma_start(out=hs[i][:], in_=src_ap), reads=src_bufs, writes=[B_hs[i]])
        return hs[i], B_hs[i]

    def transpose_to(src, B_src, kcn, gcol, dstT, B_dstT, st):
        for k0 in range(0, kcn, 8):
            nk = min(8, kcn - k0)
            pt, B_pt = next_ps()
            ptb = pt[:].bitcast(BF16)
            for j in range(nk):
                kc = k0 + j
                P.op("pe", lambda e, j=j, kc=kc, ptb=ptb: e.transpose(ptb[:, j * 128:(j + 1) * 128], src[:, kc * 128:(kc + 1) * 128], ident[:]),
                     [B_src, B_const], [B_pt])
            for j in range(nk):
                kc = k0 + j
                o = dstT[:, kc, st * 128:(st + 1) * 128]
                if gcol is None:
                    cp("dve", o, ptb[:, j * 128:(j + 1) * 128], [B_pt], [B_dstT])
                else:
                    ts("dve", o, ptb[:, j * 128:(j + 1) * 128], gcol[:, kc:kc + 1], None, ALU.mult, None, [B_pt, B_const], [B_dstT])

    def norm_T(src_rows, src_bufs_fn, gcol, dstT, B_dstT, nst):
        for st in range(nst):
            h_t, B_h = load_rows(src_rows(st), src_bufs_fn(st))
            ss = next_stat()
            act(xn[:], h_t[:], AF.Square, [B_h], [B_xn, B_stat], accum_out=ss)
            rsqrt_ap(ss, ss, D, [B_stat], [B_stat])
            act(xn[:], h_t[:], AF.Copy, [B_h, B_stat], [B_xn], scale=ss)
            transpose_to(xn, B_xn, KC, gcol, dstT, B_dstT, st)

    def fm_group(n, g, rhsT, B_rhs, ncols, evac, chunk_w=128):
        Kd, Nd, gw = WSPECS[n]
        kcn = Kd // 128
        nj = gw // chunk_w
        banks = [next_ps() for _ in range(nj)]
        nkp = max(1, min(kcn, SLOT // gw))
        for k0 in range(0, kcn, nkp):
            nk = min(nkp, kcn - k0)
            w, B_w = piece(n, g, k0, nk)
            for k in range(nk):
                kc = k0 + k
                for j in range(nj):
                    mm(banks[j][0][0:chunk_w, 0:ncols], w[:, k, j * chunk_w:(j + 1) * chunk_w], rhsT[:, kc, 0:ncols],
                       kc == 0, kc == kcn - 1, [B_w, B_rhs], [banks[j][1]])
        for j in range(nj):
            evac(g * nj + j, banks[j][0], banks[j][1])

    def tm_matmul(n, lhsT_fn, B_lhs, nst, evac):
        Kd, Nd, gw = WSPECS[n]
        kcn = Kd // 128
        nkp = SLOT // gw
        for cg in range(Nd // gw):
            banks = [next_ps() for _ in range(nst)]
            for k0 in range(0, kcn, nkp):
                nk = min(nkp, kcn - k0)
                w, B_w = piece(n, cg, k0, nk)
                for k in range(nk):
                    kc = k0 + k
                    for st in range(nst):
                        mm(banks[st][0][:, 0:gw], lhsT_fn(kc, st), w[:, k, :], kc == 0, kc == kcn - 1,
                           [B_w] + B_lhs, [banks[st][1]])
            for st in range(nst):
                evac(cg, st, banks[st][0], banks[st][1])

    def post_norm_residual(ytile, B_y, rows_in, bufs_in_fn, gname, rows_out, bufs_out_fn, ssq):
        P.dma("act", lambda e: e.dma_start(out=grow[:], in_=vec[gname].partition_broadcast(128)), reads=[B_in], writes=[B_grow])
        for st in range(4):
            ss = next_stat()
            P.op("dve", lambda e, st=st, ss=ss: e.reduce_sum(out=ss, in_=ssq[:, st * 4:(st + 1) * 4], axis=AX.X), [B_stat], [B_stat])
            rsqrt_ap(ss, ss, D, [B_stat], [B_stat])
            h_t, B_h = load_rows(rows_in(st), bufs_in_fn(st))
            stt(ytile[:, st, :], ytile[:, st, :], ss, grow[:], ALU.mult, ALU.mult, [B_y[st], B_stat, B_grow], [B_y[st]])
            tt("dve", h_t[:], h_t[:], ytile[:, st, :], ALU.add, [B_y[st], B_h], [B_h])
            P.dma("act", lambda e, st=st, h_t=h_t: e.dma_start(out=rows_out(st), in_=h_t[:]), reads=[B_h],
                  writes=bufs_out_fn(st), kind="r")

    def y_evac(ytile, B_y, ssq):
        def ev(cg, st, bank, B_bank):
            act(ytile[:, st, cg * 512:(cg + 1) * 512], bank[:], AF.Copy, [B_bank], [B_y[st]])
            act(junk[:, 0:512], bank[:], AF.Square, [B_bank], [B_stat], accum_out=ssq[:, st * 4 + cg:st * 4 + cg + 1])
        return ev

    def wo_ffn(es_outer, mix_scope, mixT, B_mix, xnT, B_xnT, ytile, B_y, rows_in, bufs_in, rows_mid, bufs_mid, rows_out, bufs_out):
        tm_matmul("w_o%d" % L[0], lambda kc, st: mixT[:, kc, st * 128:(st + 1) * 128], B_mix, 4, y_evac(ytile, B_y, ssqA))
        post_norm_residual(ytile, B_y, rows_in, bufs_in, "gr_mix_post%d" % L[0], rows_mid, bufs_mid, ssqA)
        P.barrier()
        mix_scope.close()
        norm_T(rows_mid, bufs_mid, svec["gc_mlp_pre%d" % L[0]], xnT, B_xnT, 4)
        with nc.sbuf_tensor("s_fT_%d" % P.cnt["pe"], [128, DFF // 128, T], BF16) as fT:
            B_fT = [Buf("fT%d" % i) for i in range(DFF // 128)]

            def ev1(j, bank, B_bank):
                act(fT[:, j, :], bank[:], AF.Relu, [B_bank], [B_fT[j]])
                tt("dve", fT[:, j, :], fT[:, j, :], fT[:, j, :], ALU.mult, [B_fT[j]], [B_fT[j]])
            for g in range(DFF // 512):
                fm_group("w_ff1_%d" % L[0], g, xnT, B_xnT, T, ev1)
            tm_matmul("w_ff2_%d" % L[0], lambda kc, st: fT[:, kc, st * 128:(st + 1) * 128], B_fT, 4, y_evac(ytile, B_y, ssqB))
            post_norm_residual(ytile, B_y, rows_mid, bufs_mid, "gr_mlp_post%d" % L[0], rows_out, bufs_out, ssqB)
            P.barrier()

    def mem_kv():
        with nc.sbuf_tensor("s_mnT_%d" % P.cnt["pe"], [128, KC, MEML], BF16) as mnT:
            B_mnT = Buf("mnT")
            norm_T(lambda st: mem_d[st * 128:(st + 1) * 128, :], lambda st: [B_in], svec["gc_mem%d" % L[0]], mnT, B_mnT, 2)

            def evk(j, bank, B_bank):
                act(mkT[:, j, :], bank[:, 0:MEML], AF.Copy, [B_bank], [B_mkT])
            fm_group("w_mem_k%d" % L[0], 0, mnT, B_mnT, MEML, evk)

            def evv(cg, st, bank, B_bank):
                act(mvt[:, st, :], bank[:], AF.Copy, [B_bank], [B_mvt])
            tm_matmul("w_mem_v%d" % L[0], lambda kc, st: mnT[:, kc, st * 128:(st + 1) * 128], [B_mnT], 2, evv)
            P.barrier()

    def attend(qparts, kparts, v_fn, ones_fn, nchunks, col0_fn, scale, bias, out_ap, B_out_l, reads_q, reads_kv, diag_fn=None):
        o_ps, B_o = next_ps()
        d_ps, B_d = next_ps()
        LA = 2
        pend = {}
        for c in range(nchunks + LA):
            if c < nchunks:
                c0 = col0_fn(c)
                s_ps, B_s = next_ps()
                while B_s is B_o or B_s is B_d:
                    s_ps, B_s = next_ps()
                for i, (qa, kf) in enumerate(zip(qparts, kparts)):
                    mm(s_ps[:, c0:T], kf(c), qa[:, c0:T], i == 0, i == len(qparts) - 1, reads_q + reads_kv, [B_s])
                pi = pT_i[0] % 4
                pT_i[0] += 1
                act(pT[pi][:, c0:T], s_ps[:, c0:T], AF.Exp, [B_s, B_const], [B_pT[pi]], scale=scale, bias=bias)
                if diag_fn is not None and diag_fn(c):
                    P.op("dve", lambda e, pi=pi, c0=c0: e.memset(pT[pi][64:128, c0:c0 + 64], 0.0), [], [B_pT[pi]])
                pend[c] = (pi, c0)
            cc = c - LA
            if cc >= 0:
                pi, c0 = pend.pop(cc)
                mm(o_ps[:, c0:T], v_fn(cc), pT[pi][:, c0:T], cc == 0, cc == nchunks - 1, [B_pT[pi]] + reads_kv, [B_o])
                mm(d_ps[:, c0:T], ones_fn(cc), pT[pi][:, c0:T], cc == 0, cc == nchunks - 1, [B_pT[pi], B_const], [B_d])
        P.op("dve", lambda e: e.reciprocal(out=rec[:], in_=d_ps[:]), [B_d], [B_rec])
        tt("dve", out_ap, o_ps[:], rec[:], ALU.mult, [B_o, B_rec], B_out_l)

    def mem_attend(qmT, B_qm, out_fn, B_out_fn):
        for h in range(4):
            attend([qmT[:, h, :]], [lambda c, h=h: mkT[:, h, c * 128:(c + 1) * 128]],
                   lambda c, h=h: mvt[:, c, h * 128:(h + 1) * 128], lambda c: ones_bf[:], 2, lambda c: 0,
                   128.0 ** -0.5, zbias[:, 0:1], out_fn(h), B_out_fn(h), [B_qm], [B_mkT, B_mvt])

    def rope_tables(t0, n, cosT, sinT, B_tab, tmp, tmpi, B_tmp):
        for which, dst in ((0, sinT), (1, cosT)):
            a = tmp[:, 0, 0:n]; q = tmp[:, 1, 0:n]
            ts("dve", a, posf[:, t0:t0 + n], invf[:, 0:1], (math.pi / 2 if which else 0.0), ALU.mult, ALU.add,
               [B_const], [B_tmp])
            ts("dve", q, a, 1.0 / TWO_PI, None, ALU.mult, None, [B_tmp], [B_tmp])
            cp("dve", tmpi[:, 0:n], q, [B_tmp], [B_tmp])
            cp("dve", q, tmpi[:, 0:n], [B_tmp], [B_tmp])
            stt(a, q, -TWO_PI, a, ALU.mult, ALU.add, [B_tmp], [B_tmp])
            ts("dve", a, a, math.pi, -math.pi, ALU.min, ALU.max, [B_tmp], [B_tmp])
            act(dst[:, 0:n], a, AF.Sin, [B_tmp], [B_tab])

    def rope64(x, B_x, cos, sin, B_tab, out, B_o, t1, t2, B_t):
        tt("dve", t1[0:32], x[0:32], cos[0:32], ALU.mult, [B_x, B_tab], [B_t])
        tt("dve", t2[0:32], x[32:64], sin[32:64], ALU.mult, [B_x, B_tab], [B_t])
        tt("dve", out[0:32], t1[0:32], t2[0:32], ALU.subtract, [B_t], [B_o])
        tt("dve", t1[32:64], x[32:64], cos[32:64], ALU.mult, [B_x, B_tab], [B_t])
        tt("dve", t2[32:64], x[0:32], sin[0:32], ALU.mult, [B_x, B_tab], [B_t])
        tt("dve", out[32:64], t1[32:64], t2[32:64], ALU.add, [B_t], [B_o])

    if doA:
        es = ExitStack()

        def sbl(name, shape, dt=F32, stack=None):
            uniq[0] += 1
            return (stack or es).enter_context(nc.sbuf_tensor("s_%s_%d" % (name, uniq[0]), list(shape), dt))

        wr = sbl("wr", [128, NCH, 128], BF16); wi = sbl("wi", [128, NCH, 128], BF16)
        B_gw = Buf("gatew")
        with nc.sbuf_tensor("s_wr_f", [128, NCH * 128], F32) as wr_f, nc.sbuf_tensor("s_wi_f", [128, NCH * 128], F32) as wi_f:
            P.dma("sp", lambda e: e.dma_start(out=wr_f[:], in_=wr_d), writes=[B_gw])
            P.dma("sp", lambda e: e.dma_start(out=wi_f[:], in_=wi_d), writes=[B_gw])
            cp("dve", wr[:].rearrange("p c k -> p (c k)"), wr_f[:], [B_gw], [B_gw])
            cp("dve", wi[:].rearrange("p c k -> p (c k)"), wi_f[:], [B_gw], [B_gw])
            P.barrier()
        cs = sbl("cs", [128, NCH]); nbr = sbl("nbr", [128, NCH]); nbi = sbl("nbi", [128, NCH])
        act(cs[:], svec["lam"][:], AF.Exp, [B_const], [B_const], scale=-1.0)
        act(cs[:], cs[:], AF.Ln, [B_const], [B_const], bias=1.0)
        ts("dve", cs[:], cs[:], -8.0, None, ALU.mult, None, [B_const], [B_const])
        ts("dve", nbr[:], svec["b_r"][:], -1.0, None, ALU.mult, None, [B_const], [B_const])
        ts("dve", nbi[:], svec["b_i"][:], -1.0, None, ALU.mult, None, [B_const], [B_const])
        state = sbl("state", [128, NCH]); B_state = Buf("state")
        xtail = sbl("xtail", [128, NCH, 3]); B_xtail = Buf("xtail")
        P.op("dve", lambda e: e.memset(state[:], 0.0), [], [B_state])
        P.op("dve", lambda e: e.memset(xtail[:], 0.0), [], [B_xtail])
        xnT = sbl("xnT", [128, KC, T], BF16); B_xnT = Buf("xnT")
        ytile = sbl("ytile", [128, 4, D]); B_y = [Buf("y%d" % i) for i in range(4)]
        NLT = 12
        lt_i = [0]

        def mixer_scope():
            m = ExitStack()
            o = {}
            o["mixT"] = sbl("mixT", [128, 16, T], BF16, m)
            o["gT"] = sbl("gT", [128, NCH, T], BF16, m)
            o["qmT"] = sbl("qmT", [128, 4, T], BF16, m)
            o["xbuf"] = [sbl("xbuf%d" % i, [128, 3 + T], F32, m) for i in range(4)]
            o["lt"] = [sbl("lt%d" % i, [128, T], F32, m) for i in range(NLT)]
            o["xcb"] = [sbl("xcb%d" % i, [128, T], BF16, m) for i in range(2)]
            o["B_mixT"] = [Buf("mixT%d" % i) for i in range(16)]
            o["B_gT"] = [Buf("gT%d" % i) for i in range(NCH)]
            o["B_qmT"] = Buf("qmT")
            o["B_xbuf"] = [Buf("xbuf%d" % i) for i in range(4)]
            o["B_lt"] = [Buf("lt%d" % i) for i in range(NLT)]
            o["B_xcb"] = [Buf("xcb%d" % i) for i in range(2)]
            return m, o

        def new_lt(o):
            i = lt_i[0] % NLT
            lt_i[0] += 1
            return o["lt"][i], o["B_lt"][i]

        def lru_chunk(o, c, bank, B_bank, full):
            xi = c % 4
            X, B_X = o["xbuf"][xi], o["B_xbuf"][xi]
            cp("dve", X[:, 0:3], xtail[:, c, :], [B_xtail], [B_X])
            act(X[:, 3:3 + T], bank[:], AF.Copy, [B_bank], [B_X])
            cp("dve", xtail[:, c, :], X[:, T:T + 3], [B_X], [B_xtail])
            yield
            cw = svec["conv_w"]
            xc, B_xc = new_lt(o)
            ts("dve", xc[:], X[:, 0:T], cw[:, c * 4:c * 4 + 1], svec["conv_b"][:, c:c + 1], ALU.mult, ALU.add, [B_X, B_const], [B_xc])
            for j in range(1, 4):
                stt(xc[:], X[:, j:j + T], cw[:, c * 4 + j:c * 4 + j + 1], xc[:], ALU.mult, ALU.add, [B_X, B_const, B_xc], [B_xc])
            yield
            xb_i = c % 2
            xcb, B_xcb = o["xcb"][xb_i], o["B_xcb"][xb_i]
            act(xcb[:], xc[:], AF.Copy, [B_xc], [B_xcb])
            yield
            r_ps, B_r = next_ps()
            i_ps, B_i = next_ps()
            mm(r_ps[:], wr[:, c, :], xcb[:], True, True, [B_gw, B_xcb], [B_r])
            mm(i_ps[:], wi[:, c, :], xcb[:], True, True, [B_gw, B_xcb], [B_i])
            yield
            r, B_rr = new_lt(o)
            ig, B_ig = new_lt(o)
            act(r[:], r_ps[:], AF.Sigmoid, [B_r, B_const], [B_rr], bias=svec["b_r"][:, c:c + 1])
            act(ig[:], i_ps[:], AF.Sigmoid, [B_i, B_const], [B_ig], bias=svec["b_i"][:, c:c + 1])
            yield
            a, B_a = new_lt(o)
            act(a[:], r[:], AF.Exp, [B_rr, B_const], [B_a], scale=cs[:, c:c + 1])
            tt("dve", ig[:], ig[:], xc[:], ALU.mult, [B_ig, B_xc], [B_ig])
            yield
            stt(r[:], a[:], -1.0, a[:], ALU.mult, ALU.mult, [B_a], [B_rr])
            yield
            act(r[:], r[:], AF.Ln, [B_rr], [B_rr], bias=1.0)
            act(r[:], r[:], AF.Exp, [B_rr], [B_rr], scale=0.5)
            yield
            tt("dve", ig[:], ig[:], r[:], ALU.mult, [B_ig, B_rr], [B_ig])
            hh, B_hh = new_lt(o)
            P.op("dve", lambda e: e.tensor_tensor_scan(out=hh[:], data0=a[:], data1=ig[:], initial=state[:, c:c + 1],
                                                       op0=ALU.mult, op1=ALU.add), [B_a, B_ig, B_state], [B_hh])
            cp("dve", state[:, c:c + 1], hh[:, T - 1:T], [B_hh], [B_state])
            if full:
                tt("dve", o["mixT"][:, c, :], hh[:], o["gT"][:, c, :], ALU.mult, [B_hh, o["B_gT"][c]], [o["B_mixT"][c]])

        def gelu_chunk(o, c, bank, B_bank):
            x, B_x = new_lt(o)
            act(x[:], bank[:], AF.Copy, [B_bank], [B_x])
            yield
            u, B_u = new_lt(o)
            tt("dve", u[:], x[:], x[:], ALU.mult, [B_x], [B_u])
            ts("dve", u[:], u[:], 0.044715, 1.0, ALU.mult, ALU.add, [B_u], [B_u])
            tt("dve", u[:], u[:], x[:], ALU.mult, [B_u, B_x], [B_u])
            yield
            act(u[:], u[:], AF.Sigmoid, [B_u], [B_u], scale=2.0 * math.sqrt(2.0 / math.pi))
            yield
            tt("dve", o["gT"][:, c, :], x[:], u[:], ALU.mult, [B_x, B_u], [o["B_gT"][c]])

        def interleave(gens, width=2):
            for i in range(0, len(gens), width):
                live = list(gens[i:i + width])
                while live:
                    nxt = []
                    for g_ in live:
                        try:
                            next(g_)
                            nxt.append(g_)
                        except StopIteration:
                            pass
                    live = nxt

        def group_then(n, g, fn):
            items = []
            fm_group(n, g, xnT, B_xnT, T, lambda j, bank, B_bank: items.append((j, bank, B_bank)))
            interleave([fn(j, bank, B_bank) for (j, bank, B_bank) in items])

        m, o = mixer_scope()
        stage("precast")
        for t in range(NT if dbg != "kvonly" else 0):
            norm_T(lambda st, t=t: x_prev[t * T + st * 128: t * T + (st + 1) * 128, :], lambda st: [B_in],
                   svec["gc_mix_pre%d" % L[0]], xnT, B_xnT, 4)
            stage("norm0", [("xnT", xnT[:], [B_xnT]), ("xn", xn[:], [B_xn]), ("hs1", hs[1][:], [B_hs[1]]), ("stat", stat[:], [B_stat]), ("ident", ident[:], [B_const])])
            for g in range(3):
                group_then("a_w_in", g, lambda j, bank, B_bank: lru_chunk(o, j, bank, B_bank, False))
            stage("prev0", [("state", state[:], [B_state]), ("xtail", xtail[:], [B_xtail])])
        ts("dve", state[:], state[:], flag[:, 0:1], None, ALU.mult, None, [B_state, B_const], [B_state])
        ts("dve", xtail[:].rearrange("p c k -> p (c k)"), xtail[:].rearrange("p c k -> p (c k)"), flag[:, 0:1], None,
           ALU.mult, None, [B_xtail, B_const], [B_xtail])
        stage("prev", [("state", state[:], [B_state]), ("xtail", xtail[:], [B_xtail])])
        precast(["w_ff1_0", "w_ff2_0", "kv_w_down"])
        P.barrier()
        m.close()
        if dbg != "kvonly":
            mem_kv()
        stage("memkv", [("mkT", mkT[:], [B_mkT]), ("mvt", mvt[:], [B_mvt])])
        for t in range(NT):
            rows_x = lambda st, t=t: x_own[t * T + st * 128: t * T + (st + 1) * 128, :]
            rows_h = lambda st, t=t: h1[t * T + st * 128: t * T + (st + 1) * 128, :]
            bufs_h = lambda st, t=t: [B_h1[t * 4 + st]]
            if dbg != "kvonly":
                m, o = mixer_scope()
                norm_T(rows_x, lambda st: [B_in], svec["gc_mix_pre%d" % L[0]], xnT, B_xnT, 4)
                for g in (3, 4, 5):
                    group_then("a_w_in", g, lambda j, bank, B_bank: gelu_chunk(o, j - 12, bank, B_bank))
                fm_group("a_w_in", 6, xnT, B_xnT, T,
                         lambda j, bank, B_bank: act(o["qmT"][:, j - 24, :], bank[:], AF.Copy, [B_bank], [o["B_qmT"]]))
                mem_attend(o["qmT"], o["B_qmT"], lambda h: o["mixT"][:, 12 + h, :], lambda h: [o["B_mixT"][12 + h]])
                for g in range(3):
                    group_then("a_w_in", g, lambda j, bank, B_bank: lru_chunk(o, j, bank, B_bank, True))
                stage("mixer0", [("mixT", o["mixT"][:], o["B_mixT"]), ("gT", o["gT"][:], o["B_gT"]), ("qmT", o["qmT"][:], [o["B_qmT"]])])
                wo_ffn(es, m, o["mixT"], o["B_mixT"], xnT, B_xnT, ytile, B_y, rows_x, lambda st: [B_in], rows_h, bufs_h, rows_h, bufs_h)
                stage("ffn0")
            else:
                rows_h = rows_x
                bufs_h = lambda st: [B_in]
            with ExitStack() as ks:
                latn = sbl("latn", [128, KVL], BF16, ks); B_latn = Buf("latn")
                kpef = sbl("kpef", [128, ROPE], F32, ks); B_kpef = Buf("kpef")
                glat = sbl("glat", [128, KVL], F32, ks); B_glat = Buf("glat")
                latT = sbl("latT", [128, 4, T], BF16, ks); B_latT = Buf("latT")
                kpeT = sbl("kpeT", [64, T], BF16, ks); B_kpeT = Buf("kpeT")
                cosT = sbl("cosT", [64, T], F32, ks); sinT = sbl("sinT", [64, T], F32, ks); B_tab = Buf("tab")
                rt1 = sbl("rt1", [64, T], F32, ks); rt2 = sbl("rt2", [64, T], F32, ks); B_rt = Buf("rt")
                kx = sbl("kx", [64, T], F32, ks); B_kx = Buf("kx")
                rtmp = sbl("rtmp", [64, 2, T], F32, ks); rtmpi = sbl("rtmpi", [64, T], I32, ks); B_rtmp = Buf("rtmp")
                P.dma("act", lambda e, glat=glat: e.dma_start(out=glat[:], in_=vec["gr_kv_lat"].partition_broadcast(128)), reads=[B_in],
                      writes=[B_glat])
                norm_T(rows_h, bufs_h, svec["gc_kv_in"], xnT, B_xnT, 4)
                stage("kv_a", [("xnT", xnT[:], [B_xnT])])
                rope_tables(t * T, T, cosT, sinT, B_tab, rtmp, rtmpi, B_rtmp)
                stage("kv_b", [("cosT", cosT[:], [B_tab]), ("sinT", sinT[:], [B_tab])])
                b1 = [(ps[i], B_ps[i]) for i in range(4)]
                b2 = [(ps[4 + i], B_ps[4 + i]) for i in range(4)]
                for k0 in range(0, KC, 4):
                    w, B_w = piece("kv_w_down", 0, k0, 4)
                    for k in range(4):
                        kc = k0 + k
                        for st in range(4):
                            mm(b1[st][0][:, 0:KVL], xnT[:, kc, st * 128:(st + 1) * 128], w[:, k, 0:KVL], kc == 0, kc == KC - 1,
                               [B_w, B_xnT], [b1[st][1]])
                            mm(b2[st][0][:, 0:ROPE], xnT[:, kc, st * 128:(st + 1) * 128], w[:, k, KVL:KVL + ROPE], kc == 0,
                               kc == KC - 1, [B_w, B_xnT], [b2[st][1]])
                for st in range(4):
                    ss = next_stat()
                    act(junk[:, 0:KVL], b1[st][0][:, 0:KVL], AF.Square, [b1[st][1]], [B_stat], accum_out=ss)
                    rsqrt_ap(ss, ss, KVL, [B_stat], [B_stat])
                    stt(latn[:], b1[st][0][:, 0:KVL], ss, glat[:], ALU.mult, ALU.mult, [b1[st][1], B_stat, B_glat], [B_latn])
                    act(kpef[:], b2[st][0][:, 0:ROPE], AF.Copy, [b2[st][1]], [B_kpef])
                    pt, B_pt = b1[st]
                    ptb = pt[:].bitcast(BF16)
                    for j in range(4):
                        P.op("pe", lambda e, j=j, ptb=ptb, latn=latn: e.transpose(ptb[:, j * 128:(j + 1) * 128], latn[:, j * 128:(j + 1) * 128], ident[:]),
                             [B_latn, B_const], [B_pt])
                    for j in range(4):
                        cp("dve", latT[:, j, st * 128:(st + 1) * 128], ptb[:, j * 128:(j + 1) * 128], [B_pt], [B_latT])
                    pt2, B_pt2 = b2[st]
                    P.op("pe", lambda e, pt2=pt2, kpef=kpef: e.transpose(pt2[0:64, 0:128], kpef[:], ident_f[:]), [B_kpef, B_const], [B_pt2])
                    cp("dve", kx[:, st * 128:(st + 1) * 128], pt2[0:64, 0:128], [B_pt2], [B_kx])
                stage("kv_c", [("latT", latT[:], [B_latT]), ("kx", kx[:], [B_kx])])
                rope64(kx, B_kx, cosT, sinT, B_tab, kpeT, B_kpeT, rt1, rt2, B_rt)
                P.dma("act", lambda e, t=t, latT=latT: e.dma_start(
                    out=kvx_own[t, 0:512, :].rearrange("(c p) t -> p c t", p=128), in_=latT[:]), reads=[B_latT],
                      writes=[B_kvx[t]], kind="r")
                P.dma("act", lambda e, t=t, kpeT=kpeT: e.dma_start(out=kvx_own[t, 512:KVR, :], in_=kpeT[:]), reads=[B_kpeT],
                      writes=[B_kvx[t]], kind="r")
                if mode == "F":
                    P.dma("pool", lambda e, t=t: e.collective_compute("AllGather", ALU.bypass,
                                                                      replica_groups=[[0, 1], [2, 3], [4, 5], [6, 7]],
                                                                      ins=[kvx_own[t]], outs=[kv_all[t]]),
                          reads=[B_kvx[t]], writes=[B_kv_all[t]], sem_buf=B_kv_all[t], inc=1)
                    P.bg_sems.append(B_kv_all[t].wsem)
                P.barrier()
                stage("tile0")
            if mode == "F":
                g8 = [B_h1[t * 4 + 3]]
                if t == 0:
                    precast(["w_mem_k1", "w_mem_v1", "b_w_in", "b_w_qb", "kv_w_up"], g8)
                elif t == 1:
                    precast(["w_o1", "w_ff1_1"], g8)
                elif t == 2:
                    precast(["w_ff2_1"], g8)
        es.close()

    if mode == "F" and stop == "cc":
        with nc.sbuf_tensor("s_ccdump", [128, 9 * NT, T], BF16) as ccd:
            B_ccd = Buf("ccd")
            for n_ in range(NT):
                P.dma("sp", lambda e, n_=n_: e.dma_start(out=ccd[:, n_ * 9:(n_ + 1) * 9, :], in_=kv_all[n_].rearrange("(p k) c -> p k c", k=9)),
                      reads=B_kv_all, writes=[B_ccd])
            stage("cc", [("kvall", ccd[:], [B_ccd])])
    if doB:
        L[0] = 1
        es = ExitStack()

        def sbl(name, shape, dt=F32, stack=None):
            uniq[0] += 1
            return (stack or es).enter_context(nc.sbuf_tensor("s_%s_%d" % (name, uniq[0]), list(shape), dt))

        SC = 192.0 ** -0.5
        cq_scope = ExitStack()
        cqT = sbl("cqT", [128, 6, S_CORE], BF16, cq_scope); B_cqT = Buf("cqT")
        mem_kv()
        with ExitStack() as b1s:
            xnT = sbl("xnT", [128, KC, T], BF16, b1s); B_xnT = Buf("xnT")
            cq = sbl("cq", [128, 6, T], F32, b1s); B_cq = [Buf("cq%d" % i) for i in range(6)]
            sq = sbl("sq", [128, T], BF16, b1s); B_sq = Buf("sq")
            rb = sbl("rb", [128, T], F32, b1s); B_rb = Buf("rb")
            qmT = sbl("qmT", [128, 4, T], BF16, b1s); B_qmT = Buf("qmT")
            mo = sbl("mo", [128, 4, T], BF16, b1s); B_mo = [Buf("mo%d" % i) for i in range(4)]
            for t in range(NT):
                rows_h = lambda st, t=t: h1[t * T + st * 128: t * T + (st + 1) * 128, :]
                norm_T(rows_h, lambda st, t=t: [B_h1[t * 4 + st]], svec["gc_mix_pre%d" % L[0]], xnT, B_xnT, 4)
                for g in range(3):
                    fm_group("b_w_in", g, xnT, B_xnT, T,
                             lambda j, bank, B_bank: act(cq[:, j, :], bank[:], AF.Copy, [B_bank], [B_cq[j]]))
                s_ps, B_s = next_ps()
                for j in range(6):
                    tt("dve", sq[:], cq[:, j, :], cq[:, j, :], ALU.mult, [B_cq[j]], [B_sq])
                    mm(s_ps[:], ones_bf[:], sq[:], j == 0, j == 5, [B_sq, B_const], [B_s])
                ts("dve", rb[:], s_ps[:], 1.0 / QL, EPS, ALU.mult, ALU.add, [B_s], [B_rb])
                act(rb[:], rb[:], AF.Ln, [B_rb], [B_rb])
                act(rb[:], rb[:], AF.Exp, [B_rb], [B_rb], scale=-0.5)
                for j in range(6):
                    stt(cqT[:, j, t * T:(t + 1) * T], cq[:, j, :], svec["gc_qa"][:, j:j + 1], rb[:], ALU.mult, ALU.mult,
                        [B_cq[j], B_rb, B_const], [B_cqT])
                for g in (3, 4):
                    fm_group("b_w_in", g, xnT, B_xnT, T,
                             lambda j, bank, B_bank: act(qmT[:, j - 6, :], bank[:], AF.Copy, [B_bank], [B_qmT]))
                mem_attend(qmT, B_qmT, lambda h: mo[:, h, :], lambda h: [B_mo[h]])
                for h in range(4):
                    P.dma("act", lambda e, h=h, t=t: e.dma_start(out=mixd[12 + h, :, t * T:(t + 1) * T], in_=mo[:, h, :]),
                          reads=[B_mo[h]], writes=[B_mixd[12 + h]], kind="r")
            P.barrier()
        with ExitStack() as b3s:
            cos64 = sbl("cos64", [64, S_CORE], F32, b3s); sin64 = sbl("sin64", [64, S_CORE], F32, b3s); B_tab = Buf("tab")
            with ExitStack() as rs:
                rtmp = sbl("rtmp", [64, 2, T], F32, rs); rtmpi = sbl("rtmpi", [64, T], I32, rs); B_rtmp = Buf("rtmp")
                for t in range(NT):
                    rope_tables(t * T, T, cos64[:, t * T:(t + 1) * T], sin64[:, t * T:(t + 1) * T], B_tab, rtmp, rtmpi, B_rtmp)
                P.barrier()
            latA = sbl("latA", [128, 4, 2 * S_CORE], BF16, b3s); B_latA = Buf("latA")
            kpeA = sbl("kpeA", [64, 2 * S_CORE], BF16, b3s); B_kpeA = Buf("kpeA")
            for t in range(NT):
                P.dma("act", lambda e, t=t: e.dma_start(out=latA[:, :, t * T:(t + 1) * T],
                                                        in_=kv_all[t, 0:512, :].rearrange("(c p) t -> p c t", p=128)),
                      reads=[B_kv_all[t]], writes=[B_latA])
                P.dma("act", lambda e, t=t: e.dma_start(out=latA[:, :, S_CORE + t * T:S_CORE + (t + 1) * T],
                                                        in_=kvx_own[t, 0:512, :].rearrange("(c p) t -> p c t", p=128)),
                      reads=[B_kvx[t]], writes=[B_latA])
                P.dma("act", lambda e, t=t: e.dma_start(out=kpeA[:, t * T:(t + 1) * T], in_=kv_all[t, 512:KVR, :]),
                      reads=[B_kv_all[t]], writes=[B_kpeA])
                P.dma("act", lambda e, t=t: e.dma_start(out=kpeA[:, S_CORE + t * T:S_CORE + (t + 1) * T], in_=kvx_own[t, 512:KVR, :]),
                      reads=[B_kvx[t]], writes=[B_kpeA])
            kT = sbl("kT", [128, 2 * S_CORE], BF16, b3s); B_kT = Buf("kT")
            vt = sbl("vt", [128, 32, 128], BF16, b3s); B_vt = Buf("vt")
            qnT = sbl("qnT", [128, S_CORE], BF16, b3s); B_qnT = Buf("qnT")
            qrT = sbl("qrT", [64, S_CORE], BF16, b3s); B_qrT = Buf("qrT")
            oT = sbl("oT", [128, S_CORE], BF16, b3s); B_oT = Buf("oT")
            rt1 = sbl("rt1", [64, T], F32, b3s); rt2 = sbl("rt2", [64, T], F32, b3s); B_rt = Buf("rt")
            for h in range(NH):
                wu, B_wu = piece("kv_w_up", h, 0, 4)
                for blk in range(8):
                    bank, B_bank = next_ps()
                    for lc in range(4):
                        mm(bank[:], wu[:, lc, 0:128], latA[:, lc, blk * 512:(blk + 1) * 512], lc == 0, lc == 3,
                           [B_wu, B_latA], [B_bank])
                    act(kT[:, blk * 512:(blk + 1) * 512], bank[:], AF.Copy, [B_bank], [B_kT])
                for c4 in range(8):
                    bank, B_bank = next_ps()
                    for cc in range(4):
                        c = c4 * 4 + cc
                        for lc in range(4):
                            mm(bank[:, cc * 128:(cc + 1) * 128], latA[:, lc, c * 128:(c + 1) * 128], wu[:, lc, 128:256],
                               lc == 0, lc == 3, [B_wu, B_latA], [B_bank])
                    o_v = vt[:, c4 * 4:(c4 + 1) * 4, :].rearrange("p a b -> p (a b)")
                    if c4 < 4:
                        ts("dve", o_v, bank[:], flag[:, 0:1], None, ALU.mult, None, [B_bank, B_const], [B_vt])
                    else:
                        cp("dve", o_v, bank[:], [B_bank], [B_vt])
                wq, B_wq = piece("b_w_qb", h, 0, 6)
                for t in range(NT):
                    bn, B_bn = next_ps()
                    br, B_br = next_ps()
                    for kc in range(6):
                        mm(bn[:], wq[:, kc, 0:128], cqT[:, kc, t * T:(t + 1) * T], kc == 0, kc == 5, [B_wq, B_cqT], [B_bn])
                    for kc in range(6):
                        mm(br[0:64, :], wq[:, kc, 128:192], cqT[:, kc, t * T:(t + 1) * T], kc == 0, kc == 5, [B_wq, B_cqT], [B_br])
                    act(qnT[:, t * T:(t + 1) * T], bn[:], AF.Copy, [B_bn], [B_qnT])
                    rope64(br, B_br, cos64[:, t * T:(t + 1) * T], sin64[:, t * T:(t + 1) * T], B_tab,
                           qrT[:, t * T:(t + 1) * T], B_qrT, rt1, rt2, B_rt)
                for t in range(NT):
                    nch = 16 + 4 * (t + 1)

                    def col0(c, t=t):
                        return max(0, c - 16 - 4 * t) * 128
                    attend([qnT[:, t * T:(t + 1) * T], qrT[:, t * T:(t + 1) * T]],
                           [lambda c: kT[:, c * 128:(c + 1) * 128], lambda c: kpeA[:, c * 128:(c + 1) * 128]],
                           lambda c: vt[:, c, :], lambda c: (flagones[:] if c < 16 else ones_bf[:]), nch, col0, SC, zbias[:, 0:1],
                           oT[:, t * T:(t + 1) * T], [B_oT], [B_qnT, B_qrT], [B_kT, B_kpeA, B_vt],
                           diag_fn=lambda c, t=t: c >= 16 + 4 * t)
                P.dma("act", lambda e, h=h: e.dma_start(out=mixd[h], in_=oT[:]), reads=[B_oT], writes=[B_mixd[h]], kind="r")
            P.barrier()
        cq_scope.close()
        xnT = sbl("xnT2", [128, KC, T], BF16); B_xnT = Buf("xnT")
        ytile = sbl("ytile", [128, 4, D]); B_y = [Buf("y%d" % i) for i in range(4)]
        B_mixT = [Buf("mixT%d" % i) for i in range(16)]
        for t in range(NT):
            rows_h = lambda st, t=t: h1[t * T + st * 128: t * T + (st + 1) * 128, :]
            rows_m = lambda st, t=t: hmid[t * T + st * 128: t * T + (st + 1) * 128, :]
            rows_o = lambda st, t=t: out_d[t * T + st * 128: t * T + (st + 1) * 128, :]
            m = ExitStack()
            mixT = sbl("mixT", [128, 16, T], BF16, m)
            for kc in range(16):
                P.dma("act", lambda e, kc=kc, t=t, mixT=mixT: e.dma_start(out=mixT[:, kc, :], in_=mixd[kc, :, t * T:(t + 1) * T]),
                      reads=[B_mixd[kc]], writes=[B_mixT[kc]])
            wo_ffn(es, m, mixT, B_mixT, xnT, B_xnT, ytile, B_y, rows_h, lambda st, t=t: [B_h1[t * 4 + st]], rows_m,
                   lambda st, t=t: [B_hmid[t * 4 + st]], rows_o, lambda st, t=t: [B_outr[t * 4 + st]])
        es.close()

    P.barrier(final=True)
    P.emit()
    return nc


def _cols(v, n):
    return np.ascontiguousarray(np.asarray(v, np.float32).reshape(n, 128).T)


_NC_CACHE = {}


def _get_nc(mode):
    if mode not in _NC_CACHE:
        _NC_CACHE[mode] = build(mode)
    return _NC_CACHE[mode]


def kernel(x, mem, positions, g_mix_pre, g_mix_post, g_mlp_pre, g_mlp_post, g_mem, w_mem_k, w_mem_v, w_o, w_ff1, w_ff2,
           a_w_in, a_conv_w, a_conv_b, a_w_rgate, a_b_rgate, a_w_igate, a_b_igate, a_lambda,
           b_w_in, b_g_qa, b_w_qb, kv_g_in, kv_w_down, kv_g_latent, kv_w_up):
    f = lambda a: np.ascontiguousarray(np.asarray(a, dtype=np.float32))
    x = f(x); mem = f(mem); positions = np.asarray(positions, dtype=np.int32)
    ident = np.eye(128, dtype=np.float32)
    invf = (10000.0 ** (-np.arange(32, dtype=np.float32) / np.float32(32))).astype(np.float32)
    invf64 = np.ascontiguousarray(np.concatenate([invf, invf]).reshape(64, 1))
    ncore = 8
    common = []
    for c in range(ncore):
        b, hf = c // 2, c % 2
        pos = positions[b, hf * S_CORE:(hf + 1) * S_CORE]
        common.append({
            "ident": ident, "flag": np.full((128, 1), float(hf), np.float32), "invf": invf64,
            "pos64": np.ascontiguousarray(np.broadcast_to(pos[None, :], (64, S_CORE))).astype(np.int32),
            "mem": f(mem[b]),
        })

    def lvec(l):
        return {
            "gc_mix_pre%d" % l: _cols(g_mix_pre[l], KC), "gc_mlp_pre%d" % l: _cols(g_mlp_pre[l], KC), "gc_mem%d" % l: _cols(g_mem[l], KC),
            "gr_mix_post%d" % l: f(g_mix_post[l]).reshape(1, D), "gr_mlp_post%d" % l: f(g_mlp_post[l]).reshape(1, D),
        }

    wA = {"w_mem_k0": f(w_mem_k[0]), "w_mem_v0": f(w_mem_v[0]), "a_w_in": f(a_w_in[0]), "w_o0": f(w_o[0]),
          "w_ff1_0": f(w_ff1[0]), "w_ff2_0": f(w_ff2[0]), "kv_w_down": f(kv_w_down)}
    vA = lvec(0)
    vA.update({
        "conv_w": np.ascontiguousarray(f(a_conv_w[0]).reshape(4, NCH, 128).transpose(2, 1, 0).reshape(128, NCH * 4)),
        "conv_b": _cols(a_conv_b[0], NCH), "b_r": _cols(a_b_rgate[0], NCH), "b_i": _cols(a_b_igate[0], NCH),
        "lam": _cols(a_lambda[0], NCH), "gc_kv_in": _cols(kv_g_in, KC), "gr_kv_lat": f(kv_g_latent).reshape(1, KVL),
        "a_w_r": np.ascontiguousarray(f(a_w_rgate[0]).transpose(1, 0, 2).reshape(128, NCH * 128)),
        "a_w_i": np.ascontiguousarray(f(a_w_igate[0]).transpose(1, 0, 2).reshape(128, NCH * 128)),
    })
    wB = {"w_mem_k1": f(w_mem_k[1]), "w_mem_v1": f(w_mem_v[1]), "b_w_in": f(b_w_in[0]), "b_w_qb": f(b_w_qb[0]),
          "kv_w_up": f(kv_w_up), "w_o1": f(w_o[1]), "w_ff1_1": f(w_ff1[1]), "w_ff2_1": f(w_ff2[1])}
    vB = lvec(1)
    vB["gc_qa"] = _cols(b_g_qa[0], 6)
    zeros_prev = np.zeros((S_CORE, D), np.float32)
    in_maps = []
    for c in range(ncore):
        b, hf = c // 2, c % 2
        m = dict(common[c]); m.update(wA); m.update(vA); m.update(wB); m.update(vB)
        m["x_own"] = np.ascontiguousarray(x[b, hf * S_CORE:(hf + 1) * S_CORE])
        m["x_prev"] = np.ascontiguousarray(x[b, 0:S_CORE]) if hf == 1 else zeros_prev
        in_maps.append(m)
    res = run_bass_kernel_spmd(_get_nc("F"), in_maps, core_ids=list(range(ncore))).results
    out = np.empty((4, 2 * S_CORE, D), np.float32)
    for c in range(ncore):
        b, hf = c // 2, c % 2
        out[b, hf * S_CORE:(hf + 1) * S_CORE] = res[c]["out"]
    return out
```

```python
import math
import numpy as np
import concourse.bass as bass
import concourse.mybir as mybir
from concourse.bass_utils import run_bass_kernel_spmd

F32 = mybir.dt.float32
BF16 = mybir.dt.bfloat16
I32 = mybir.dt.int32
AF = mybir.ActivationFunctionType
ALU = mybir.AluOpType
AX = mybir.AxisListType

D = 2048
KC = 16
S_CORE = 2048
T = 512
NT = S_CORE // T
DFF = 8192
LRU = 1536
NCH = 12
MEMW = 512
MEML = 256
QL = 768
KVL = 512
ROPE = 64
NH = 12
EPS = 1e-6
SLOT = 4096
NSLOT = 4
TWO_PI = 2.0 * math.pi

ENGS = ("pe", "act", "dve", "pool", "sp")


class Buf:
    __slots__ = ("name", "lw", "rd", "wsem", "rsem")

    def __init__(self, name):
        self.name = name
        self.lw = None
        self.rd = {}
        self.wsem = None
        self.rsem = None


class Prog:
    def __init__(self, nc):
        self.nc = nc
        self.ops = {e: [] for e in ENGS}
        self.cnt = {e: 0 for e in ENGS}
        self.seen = {e: {} for e in ENGS}
        self.esem = {}
        self.dsems = []

    def _need(self, eng, dep, waits):
        if dep is None:
            return
        kind, key, val = dep
        if kind == "e" and key == "pe" and eng == "pe":
            return
        k = (kind, key if kind == "e" else id(key))
        if self.seen[eng].get(k, 0) >= val:
            return
        self.seen[eng][k] = val
        waits.append((kind, key, val))

    def _deps(self, eng, reads, writes):
        waits = []
        for b in reads:
            self._need(eng, b.lw, waits)
        for b in writes:
            self._need(eng, b.lw, waits)
            for d in b.rd.values():
                self._need(eng, d, waits)
        return waits

    def op(self, eng, fn, reads=(), writes=()):
        waits = self._deps(eng, reads, writes)
        self.cnt[eng] += 1
        dep = ("e", eng, self.cnt[eng])
        self.ops[eng].append((waits, fn, ("e", None, 1)))
        for b in reads:
            b.rd[eng] = dep
        for b in writes:
            b.lw = dep
            b.rd = {}
        return dep

    def dma(self, queue, fn, reads=(), writes=(), sem_buf=None, kind="w", inc=16):
        waits = self._deps(queue, reads, writes)
        if sem_buf is None:
            sem_buf = writes[0] if kind == "w" else reads[0]
        if kind == "w":
            if sem_buf.wsem is None:
                sem_buf.wsem = [None, 0]
                self.dsems.append(sem_buf.wsem)
            rec = sem_buf.wsem
        else:
            if sem_buf.rsem is None:
                sem_buf.rsem = [None, 0]
                self.dsems.append(sem_buf.rsem)
            rec = sem_buf.rsem
        rec[1] += inc
        dep = ("d", rec, rec[1])
        self.ops[queue].append((waits, fn, ("d", rec, inc)))
        for b in reads:
            b.rd[id(rec)] = dep
        for b in writes:
            b.lw = dep
            b.rd = {}
        return dep

    def wait_all(self, eng, bufs):
        waits = []
        for b in bufs:
            self._need(eng, b.lw, waits)
            for d in b.rd.values():
                self._need(eng, d, waits)
        self.ops[eng].append((waits, None, None))

    def barrier(self, final=False):
        skip = getattr(self, "bg_sems", ())
        snap = [("e", e, self.cnt[e]) for e in ENGS if self.cnt[e] > 0 and (final or e != "pool")]
        snap += [("d", rec, rec[1]) for rec in self.dsems if rec[1] > 0 and (final or not any(rec is x for x in skip))]
        for e in ENGS:
            if e == "pool" and not final:
                continue
            waits = []
            for d in snap:
                if d[0] == "e" and d[1] == e:
                    continue
                self._need(e, d, waits)
            self.ops[e].append((waits, None, None))

    def emit(self):
        nc = self.nc
        for e in ENGS:
            self.esem[e] = nc.alloc_semaphore("es_" + e)
        for i, rec in enumerate(self.dsems):
            rec[0] = nc.alloc_semaphore("ds_%d" % i)
        ops = self.ops
        esem = self.esem

        def run(e, eo):
            for waits, fn, inc in ops[e]:
                for kind, key, val in waits:
                    if kind == "e":
                        eo.wait_ge(esem[key], val)
                    else:
                        eo.wait_ge(key[0], val)
                if fn is None:
                    continue
                ins = fn(eo)
                if inc[0] == "e":
                    ins.then_inc(esem[e], 1)
                else:
                    ins.then_inc(inc[1][0], inc[2])

        with nc.Block() as block:
            @block.sync
            def _(eo):
                run("sp", eo)

            @block.scalar
            def _(eo):
                run("act", eo)

            @block.vector
            def _(eo):
                run("dve", eo)

            @block.gpsimd
            def _(eo):
                run("pool", eo)

            @block.tensor
            def _(eo):
                run("pe", eo)


WSPECS = {
    "w_mem_k0": (D, MEMW, 512), "w_mem_v0": (D, MEMW, 512),
    "a_w_in": (D, 3584, 512),
    "w_o0": (D, D, 512), "w_ff1_0": (D, DFF, 512), "w_ff2_0": (DFF, D, 512),
    "kv_w_down": (D, KVL + ROPE, 576),
    "w_mem_k1": (D, MEMW, 512), "w_mem_v1": (D, MEMW, 512),
    "b_w_in": (D, 1280, 256), "b_w_qb": (QL, NH * 192, 192), "kv_w_up": (KVL, NH * 256, 256),
    "w_o1": (D, D, 512), "w_ff1_1": (D, DFF, 512), "w_ff2_1": (DFF, D, 512),
}
L0_W = ["w_mem_k0", "w_mem_v0", "a_w_in", "w_o0", "w_ff1_0", "w_ff2_0", "kv_w_down"]
L1_W = ["w_mem_k1", "w_mem_v1", "b_w_in", "b_w_qb", "kv_w_up", "w_o1", "w_ff1_1", "w_ff2_1"]


class _Stop(Exception):
    pass


def build(mode, stop=None, dbg=None):
    try:
        return _build(mode, stop, dbg)
    except _Stop as s:
        return s.args[0]


def _build(mode, stop, dbg):
    from contextlib import ExitStack
    nc = bass.Bass("TRN2", target_bir_lowering=False)
    P = Prog(nc)
    uniq = [0]
    doA = mode in ("A", "F")
    doB = mode in ("B", "F")
    L = [0 if doA else 1]

    def din(name, shape, dt=F32):
        return nc.dram_tensor(name, list(shape), dt, kind="ExternalInput").ap()

    def dout(name, shape, dt=F32):
        return nc.dram_tensor(name, list(shape), dt, kind="ExternalOutput").ap()

    def dscr(name, shape, dt=F32):
        return nc.dram_tensor(name, list(shape), dt).ap()

    ident_d = din("ident", [128, 128])
    flag_d = din("flag", [128, 1])
    invf_d = din("invf", [64, 1])
    pos_d = din("pos64", [64, S_CORE], I32)
    mem_d = din("mem", [MEML, D])
    wnames = (L0_W if doA else []) + (L1_W if doB else [])
    wsrc = {n: din(n, [WSPECS[n][0], WSPECS[n][1]]) for n in wnames}
    vec = {}

    def vin(name, shape):
        vec[name] = din(name, shape)

    for l_ in ([0] if doA else []) + ([1] if doB else []):
        vin("gc_mix_pre%d" % l_, [128, KC]); vin("gc_mlp_pre%d" % l_, [128, KC]); vin("gc_mem%d" % l_, [128, KC])
        vin("gr_mix_post%d" % l_, [1, D]); vin("gr_mlp_post%d" % l_, [1, D])
    NRB = S_CORE // 128
    if doA:
        x_own = din("x_own", [S_CORE, D])
        x_prev = din("x_prev", [S_CORE, D])
        vin("conv_w", [128, NCH * 4]); vin("conv_b", [128, NCH]); vin("b_r", [128, NCH]); vin("b_i", [128, NCH])
        vin("lam", [128, NCH]); vin("gc_kv_in", [128, KC]); vin("gr_kv_lat", [1, KVL])
        wr_d = din("a_w_r", [128, NCH * 128]); wi_d = din("a_w_i", [128, NCH * 128])
    KVR = 4 * 128 + ROPE
    if mode == "A":
        h1 = dout("h1", [S_CORE, D])
        kvx_own = dout("kvx_own", [NT, KVR, T], BF16)
    elif mode == "B":
        h1 = din("h1", [S_CORE, D])
        kvx_own = din("kvx_own", [NT, KVR, T], BF16)
        kv_all = din("kv_all", [NT, 2 * KVR, T], BF16)
    else:
        h1 = dscr("h1", [S_CORE, D])
        kvx_own = dscr("kvx_own", [NT, KVR, T], BF16)
        kv_all = dscr("kv_all", [NT, 2 * KVR, T], BF16)
    B_kv_all = [Buf("kv_all%d" % i) for i in range(NT)]
    B_kvx = [Buf("kvx%d" % i) for i in range(NT)]
    if doB:
        vin("gc_qa", [128, 6])
        out_d = dout("out", [S_CORE, D])
        hmid = dscr("hmid", [S_CORE, D])
        mixd = dscr("mixd", [16, 128, S_CORE], BF16)
        B_hmid = [Buf("hmid%d" % i) for i in range(NRB)]
        B_outr = [Buf("out%d" % i) for i in range(NRB)]
        B_mixd = [Buf("mixd%d" % i) for i in range(16)]
    B_h1 = [Buf("h1_%d" % i) for i in range(NRB)]
    B_lat_own = Buf("lat_own"); B_kpe_own = Buf("kpe_own")
    B_in = Buf("inputs")
    dbg_outs = {}

    def stage(name, dumps=()):
        if stop != name:
            return
        for dn, ap, bufs in dumps:
            o = dout("dbg_" + dn, list(ap.shape), ap.dtype)
            P.dma("sp", lambda e, o=o, ap=ap: e.dma_start(out=o, in_=ap), reads=bufs, writes=[Buf("dbg")], kind="r",
                  sem_buf=Buf("dbgs"))
        P.barrier(final=True)
        P.emit()
        raise _Stop(nc)


    wb = {}
    B_wb = {}
    P.bg_sems = []
    for n in wnames:
        Kd, Nd, gw = WSPECS[n]
        wb[n] = dscr("wb_" + n, [Nd // gw, 128, (Kd // 128) * gw], BF16)
        B_wb[n] = Buf("wb_" + n)

    def precast(names, gate=()):
        for n in names:
            Kd, Nd, gw = WSPECS[n]
            kcn = Kd // 128
            for g in range(Nd // gw):
                src = wsrc[n][:, g * gw:(g + 1) * gw].rearrange("(kc p) c -> p kc c", p=128)
                dst = wb[n][g].rearrange("p (kc c) -> p kc c", c=gw)
                nsp = 2 if kcn > 32 else 1
                hk = kcn // nsp
                for hh in range(nsp):
                    P.dma("pool", lambda e, s=src[:, hh * hk:(hh + 1) * hk, :], d=dst[:, hh * hk:(hh + 1) * hk, :]:
                          e.dma_start(out=d, in_=s), reads=list(gate), writes=[B_wb[n]])
            if not any(B_wb[n].wsem is x for x in P.bg_sems):
                P.bg_sems.append(B_wb[n].wsem)

    if doA:
        precast(["a_w_in", "w_mem_k0", "w_mem_v0", "w_o0"])
    else:
        precast(L1_W)

    def sb(name, shape, dt=F32):
        return nc.alloc_sbuf_tensor("s_" + name, list(shape), dt)

    wring = [sb("wring%d" % i, [128, SLOT], BF16) for i in range(NSLOT)]
    B_wring = [Buf("wring%d" % i) for i in range(NSLOT)]
    ring_i = [0]
    ident_f = sb("ident_f", [128, 128]); ident = sb("ident", [128, 128], BF16)
    ones_bf = sb("ones_bf", [128, 128], BF16)
    flag = sb("flag", [128, 1]); flagones = sb("flagones", [128, 128], BF16)
    zbias = sb("zbias", [128, 1])
    B_const = Buf("const")
    P.dma("sp", lambda e: e.dma_start(out=ident_f[:], in_=ident_d), writes=[B_const])
    P.dma("sp", lambda e: e.dma_start(out=flag[:], in_=flag_d), writes=[B_const])
    svec = {}
    for n, ap in vec.items():
        if n.startswith("gr_"):
            continue
        svec[n] = sb("v_" + n, list(ap.shape))
        P.dma("sp", lambda e, o=svec[n], i=ap: e.dma_start(out=o[:], in_=i), writes=[B_const])
    posf = sb("posf", [64, S_CORE]); invf = sb("invf", [64, 1])
    with nc.sbuf_tensor("s_posi", [64, S_CORE], I32) as posi:
        P.dma("sp", lambda e: e.dma_start(out=posi[:], in_=pos_d), writes=[B_const])
        P.dma("sp", lambda e: e.dma_start(out=invf[:], in_=invf_d), writes=[B_const])
        P.op("dve", lambda e: e.tensor_copy(out=posf[:], in_=posi[:]), [B_const], [B_const])
        P.op("dve", lambda e: e.tensor_copy(out=ident[:], in_=ident_f[:]), [B_const], [B_const])
        P.op("dve", lambda e: e.memset(ones_bf[:], 1.0), [B_const], [B_const])
        P.op("dve", lambda e: e.memset(zbias[:], 0.0), [B_const], [B_const])
        P.op("dve", lambda e: e.tensor_scalar(out=flagones[:], in0=ones_bf[:], scalar1=flag[:, 0:1], scalar2=None, op0=ALU.mult),
             [B_const], [B_const])
        P.barrier()
    stage("init", [("posf", posf[:], [B_const]), ("flagones", flagones[:], [B_const])])
    grow = sb("grow", [128, D]); B_grow = Buf("grow")
    hs = [sb("hs%d" % i, [128, D]) for i in range(2)]; B_hs = [Buf("hs%d" % i) for i in range(2)]
    hs_i = [0]
    xn = sb("xn", [128, D], BF16); B_xn = Buf("xn")
    junk = sb("junk", [128, 512], BF16)
    stat = sb("stat", [128, 64]); B_stat = Buf("stat")
    stat_i = [0]
    ssqA = sb("ssqA", [128, 16]); ssqB = sb("ssqB", [128, 16])
    mkT = sb("mkT", [128, 4, MEML], BF16); B_mkT = Buf("mkT")
    mvt = sb("mvt", [128, 2, MEMW], BF16); B_mvt = Buf("mvt")
    pT = [sb("pT%d" % i, [128, T], BF16) for i in range(4)]; B_pT = [Buf("pT%d" % i) for i in range(4)]
    pT_i = [0]
    rec = sb("rec", [128, T]); B_rec = Buf("rec")
    ps = [nc.alloc_psum_tensor("ps%d" % i, [128, 512], F32) for i in range(8)]
    B_ps = [Buf("ps%d" % i) for i in range(8)]
    ps_i = [0]

    def next_ps():
        i = ps_i[0] % 8
        ps_i[0] += 1
        return ps[i], B_ps[i]

    def next_stat(n=1):
        if stat_i[0] + n > 64:
            stat_i[0] = 0
        a = stat[:, stat_i[0]:stat_i[0] + n]
        stat_i[0] += n
        return a

    def act(out, in_, func, reads, writes, scale=1.0, bias=0.0, accum_out=None):
        if accum_out is None:
            P.op("act", lambda e: e.activation(out=out, in_=in_, func=func, bias=bias, scale=scale), reads, writes)
        else:
            P.op("act", lambda e: e.activation(out=out, in_=in_, func=func, bias=bias, scale=scale, accum_out=accum_out),
                 reads, writes)

    def tt(eng, out, in0, in1, op, reads, writes):
        P.op(eng, lambda e: e.tensor_tensor(out=out, in0=in0, in1=in1, op=op), reads, writes)

    def ts(eng, out, in0, s1, s2, op0, op1, reads, writes):
        if s2 is None:
            P.op(eng, lambda e: e.tensor_scalar(out=out, in0=in0, scalar1=s1, scalar2=None, op0=op0), reads, writes)
        else:
            P.op(eng, lambda e: e.tensor_scalar(out=out, in0=in0, scalar1=s1, scalar2=s2, op0=op0, op1=op1), reads, writes)

    def stt(out, in0, scalar, in1, op0, op1, reads, writes):
        P.op("dve", lambda e: e.scalar_tensor_tensor(out=out, in0=in0, scalar=scalar, in1=in1, op0=op0, op1=op1),
             reads, writes)

    def cp(eng, out, in_, reads, writes):
        P.op(eng, lambda e: e.tensor_copy(out=out, in_=in_), reads, writes)

    def mm(out, lhsT, rhs, start, stop, reads, writes):
        P.op("pe", lambda e: e.matmul(out, lhsT=lhsT, rhs=rhs, start=start, stop=stop), reads, writes)

    def piece(n, g, k0, nk):
        Kd, Nd, gw = WSPECS[n]
        i = ring_i[0] % NSLOT
        ring_i[0] += 1
        dst = wring[i][:, 0:nk * gw].rearrange("p (k c) -> p k c", c=gw)
        src = wb[n][g].rearrange("p (k c) -> p k c", c=gw)[:, k0:k0 + nk, :]
        P.dma("sp", lambda e: e.dma_start(out=dst, in_=src), reads=[B_wb[n]], writes=[B_wring[i]])
        return dst, B_wring[i]

    def rsqrt_ap(dst, src, n_feat, reads, writes, B_tmp=None):
        ts("dve", dst, src, 1.0 / n_feat, EPS, ALU.mult, ALU.add, reads, writes)
        act(dst, dst, AF.Ln, writes, writes)
        act(dst, dst, AF.Exp, writes, writes, scale=-0.5)

    def load_rows(src_ap, src_bufs, queue="act"):
        i = hs_i[0] % 2
        hs_i[0] += 1
        P.dma(queue, lambda e: e.dma_start(out=hs[i][:], in_=src_ap), reads=src_bufs, writes=[B_hs[i]])
        return hs[i], B_hs[i]

    def transpose_to(src, B_src, kcn, gcol, dstT, B_dstT, st):
        for k0 in range(0, kcn, 8):
            nk = min(8, kcn - k0)
            pt, B_pt = next_ps()
            ptb = pt[:].bitcast(BF16)
            for j in range(nk):
                kc = k0 + j
                P.op("pe", lambda e, j=j, kc=kc, ptb=ptb: e.transpose(ptb[:, j * 128:(j + 1) * 128], src[:, kc * 128:(kc + 1) * 128], ident[:]),
                     [B_src, B_const], [B_pt])
            for j in range(nk):
                kc = k0 + j
                o = dstT[:, kc, st * 128:(st + 1) * 128]
                if gcol is None:
                    cp("dve", o, ptb[:, j * 128:(j + 1) * 128], [B_pt], [B_dstT])
                else:
                    ts("dve", o, ptb[:, j * 128:(j + 1) * 128], gcol[:, kc:kc + 1], None, ALU.mult, None, [B_pt, B_const], [B_dstT])

    def norm_T(src_rows, src_bufs_fn, gcol, dstT, B_dstT, nst):
        for st in range(nst):
            h_t, B_h = load_rows(src_rows(st), src_bufs_fn(st))
            ss = next_stat()
            act(xn[:], h_t[:], AF.Square, [B_h], [B_xn, B_stat], accum_out=ss)
            rsqrt_ap(ss, ss, D, [B_stat], [B_stat])
            act(xn[:], h_t[:], AF.Copy, [B_h, B_stat], [B_xn], scale=ss)
            transpose_to(xn, B_xn, KC, gcol, dstT, B_dstT, st)

    def fm_group(n, g, rhsT, B_rhs, ncols, evac, chunk_w=128):
        Kd, Nd, gw = WSPECS[n]
        kcn = Kd // 128
        nj = gw // chunk_w
        banks = [next_ps() for _ in range(nj)]
        nkp = max(1, min(kcn, SLOT // gw))
        for k0 in range(0, kcn, nkp):
            nk = min(nkp, kcn - k0)
            w, B_w = piece(n, g, k0, nk)
            for k in range(nk):
                kc = k0 + k
                for j in range(nj):
                    mm(banks[j][0][0:chunk_w, 0:ncols], w[:, k, j * chunk_w:(j + 1) * chunk_w], rhsT[:, kc, 0:ncols],
                       kc == 0, kc == kcn - 1, [B_w, B_rhs], [banks[j][1]])
        for j in range(nj):
            evac(g * nj + j, banks[j][0], banks[j][1])

    def tm_matmul(n, lhsT_fn, B_lhs, nst, evac):
        Kd, Nd, gw = WSPECS[n]
        kcn = Kd // 128
        nkp = SLOT // gw
        for cg in range(Nd // gw):
            banks = [next_ps() for _ in range(nst)]
            for k0 in range(0, kcn, nkp):
                nk = min(nkp, kcn - k0)
                w, B_w = piece(n, cg, k0, nk)
                for k in range(nk):
                    kc = k0 + k
                    for st in range(nst):
                        mm(banks[st][0][:, 0:gw], lhsT_fn(kc, st), w[:, k, :], kc == 0, kc == kcn - 1,
                           [B_w] + B_lhs, [banks[st][1]])
            for st in range(nst):
                evac(cg, st, banks[st][0], banks[st][1])

    def post_norm_residual(ytile, B_y, rows_in, bufs_in_fn, gname, rows_out, bufs_out_fn, ssq):
        P.dma("act", lambda e: e.dma_start(out=grow[:], in_=vec[gname].partition_broadcast(128)), reads=[B_in], writes=[B_grow])
        for st in range(4):
            ss = next_stat()
            P.op("dve", lambda e, st=st, ss=ss: e.reduce_sum(out=ss, in_=ssq[:, st * 4:(st + 1) * 4], axis=AX.X), [B_stat], [B_stat])
            rsqrt_ap(ss, ss, D, [B_stat], [B_stat])
            h_t, B_h = load_rows(rows_in(st), bufs_in_fn(st))
            stt(ytile[:, st, :], ytile[:, st, :], ss, grow[:], ALU.mult, ALU.mult, [B_y[st], B_stat, B_grow], [B_y[st]])
            tt("dve", h_t[:], h_t[:], ytile[:, st, :], ALU.add, [B_y[st], B_h], [B_h])
            P.dma("act", lambda e, st=st, h_t=h_t: e.dma_start(out=rows_out(st), in_=h_t[:]), reads=[B_h],
                  writes=bufs_out_fn(st), kind="r")

    def y_evac(ytile, B_y, ssq):
        def ev(cg, st, bank, B_bank):
            act(ytile[:, st, cg * 512:(cg + 1) * 512], bank[:], AF.Copy, [B_bank], [B_y[st]])
            act(junk[:, 0:512], bank[:], AF.Square, [B_bank], [B_stat], accum_out=ssq[:, st * 4 + cg:st * 4 + cg + 1])
        return ev

    def wo_ffn(es_outer, mix_scope, mixT, B_mix, xnT, B_xnT, ytile, B_y, rows_in, bufs_in, rows_mid, bufs_mid, rows_out, bufs_out):
        tm_matmul("w_o%d" % L[0], lambda kc, st: mixT[:, kc, st * 128:(st + 1) * 128], B_mix, 4, y_evac(ytile, B_y, ssqA))
        post_norm_residual(ytile, B_y, rows_in, bufs_in, "gr_mix_post%d" % L[0], rows_mid, bufs_mid, ssqA)
        P.barrier()
        mix_scope.close()
        norm_T(rows_mid, bufs_mid, svec["gc_mlp_pre%d" % L[0]], xnT, B_xnT, 4)
        with nc.sbuf_tensor("s_fT_%d" % P.cnt["pe"], [128, DFF // 128, T], BF16) as fT:
            B_fT = [Buf("fT%d" % i) for i in range(DFF // 128)]

            def ev1(j, bank, B_bank):
                act(fT[:, j, :], bank[:], AF.Relu, [B_bank], [B_fT[j]])
                tt("dve", fT[:, j, :], fT[:, j, :], fT[:, j, :], ALU.mult, [B_fT[j]], [B_fT[j]])
            for g in range(DFF // 512):
                fm_group("w_ff1_%d" % L[0], g, xnT, B_xnT, T, ev1)
            tm_matmul("w_ff2_%d" % L[0], lambda kc, st: fT[:, kc, st * 128:(st + 1) * 128], B_fT, 4, y_evac(ytile, B_y, ssqB))
            post_norm_residual(ytile, B_y, rows_mid, bufs_mid, "gr_mlp_post%d" % L[0], rows_out, bufs_out, ssqB)
            P.barrier()

    def mem_kv():
        with nc.sbuf_tensor("s_mnT_%d" % P.cnt["pe"], [128, KC, MEML], BF16) as mnT:
            B_mnT = Buf("mnT")
            norm_T(lambda st: mem_d[st * 128:(st + 1) * 128, :], lambda st: [B_in], svec["gc_mem%d" % L[0]], mnT, B_mnT, 2)

            def evk(j, bank, B_bank):
                act(mkT[:, j, :], bank[:, 0:MEML], AF.Copy, [B_bank], [B_mkT])
            fm_group("w_mem_k%d" % L[0], 0, mnT, B_mnT, MEML, evk)

            def evv(cg, st, bank, B_bank):
                act(mvt[:, st, :], bank[:], AF.Copy, [B_bank], [B_mvt])
            tm_matmul("w_mem_v%d" % L[0], lambda kc, st: mnT[:, kc, st * 128:(st + 1) * 128], [B_mnT], 2, evv)
            P.barrier()

    def attend(qparts, kparts, v_fn, ones_fn, nchunks, col0_fn, scale, bias, out_ap, B_out_l, reads_q, reads_kv, diag_fn=None):
        o_ps, B_o = next_ps()
        d_ps, B_d = next_ps()
        LA = 2
        pend = {}
        for c in range(nchunks + LA):
            if c < nchunks:
                c0 = col0_fn(c)
                s_ps, B_s = next_ps()
                while B_s is B_o or B_s is B_d:
                    s_ps, B_s = next_ps()
                for i, (qa, kf) in enumerate(zip(qparts, kparts)):
                    mm(s_ps[:, c0:T], kf(c), qa[:, c0:T], i == 0, i == len(qparts) - 1, reads_q + reads_kv, [B_s])
                pi = pT_i[0] % 4
                pT_i[0] += 1
                act(pT[pi][:, c0:T], s_ps[:, c0:T], AF.Exp, [B_s, B_const], [B_pT[pi]], scale=scale, bias=bias)
                if diag_fn is not None and diag_fn(c):
                    P.op("dve", lambda e, pi=pi, c0=c0: e.memset(pT[pi][64:128, c0:c0 + 64], 0.0), [], [B_pT[pi]])
                pend[c] = (pi, c0)
            cc = c - LA
            if cc >= 0:
                pi, c0 = pend.pop(cc)
                mm(o_ps[:, c0:T], v_fn(cc), pT[pi][:, c0:T], cc == 0, cc == nchunks - 1, [B_pT[pi]] + reads_kv, [B_o])
                mm(d_ps[:, c0:T], ones_fn(cc), pT[pi][:, c0:T], cc == 0, cc == nchunks - 1, [B_pT[pi], B_const], [B_d])
        P.op("dve", lambda e: e.reciprocal(out=rec[:], in_=d_ps[:]), [B_d], [B_rec])
        tt("dve", out_ap, o_ps[:], rec[:], ALU.mult, [B_o, B_rec], B_out_l)

    def mem_attend(qmT, B_qm, out_fn, B_out_fn):
        for h in range(4):
            attend([qmT[:, h, :]], [lambda c, h=h: mkT[:, h, c * 128:(c + 1) * 128]],
                   lambda c, h=h: mvt[:, c, h * 128:(h + 1) * 128], lambda c: ones_bf[:], 2, lambda c: 0,
                   128.0 ** -0.5, zbias[:, 0:1], out_fn(h), B_out_fn(h), [B_qm], [B_mkT, B_mvt])

    def rope_tables(t0, n, cosT, sinT, B_tab, tmp, tmpi, B_tmp):
        for which, dst in ((0, sinT), (1, cosT)):
            a = tmp[:, 0, 0:n]; q = tmp[:, 1, 0:n]
            ts("dve", a, posf[:, t0:t0 + n], invf[:, 0:1], (math.pi / 2 if which else 0.0), ALU.mult, ALU.add,
               [B_const], [B_tmp])
            ts("dve", q, a, 1.0 / TWO_PI, None, ALU.mult, None, [B_tmp], [B_tmp])
            cp("dve", tmpi[:, 0:n], q, [B_tmp], [B_tmp])
            cp("dve", q, tmpi[:, 0:n], [B_tmp], [B_tmp])
            stt(a, q, -TWO_PI, a, ALU.mult, ALU.add, [B_tmp], [B_tmp])
            ts("dve", a, a, math.pi, -math.pi, ALU.min, ALU.max, [B_tmp], [B_tmp])
            act(dst[:, 0:n], a, AF.Sin, [B_tmp], [B_tab])

    def rope64(x, B_x, cos, sin, B_tab, out, B_o, t1, t2, B_t):
        tt("dve", t1[0:32], x[0:32], cos[0:32], ALU.mult, [B_x, B_tab], [B_t])
        tt("dve", t2[0:32], x[32:64], sin[32:64], ALU.mult, [B_x, B_tab], [B_t])
        tt("dve", out[0:32], t1[0:32], t2[0:32], ALU.subtract, [B_t], [B_o])
        tt("dve", t1[32:64], x[32:64], cos[32:64], ALU.mult, [B_x, B_tab], [B_t])
        tt("dve", t2[32:64], x[0:32], sin[0:32], ALU.mult, [B_x, B_tab], [B_t])
        tt("dve", out[32:64], t1[32:64], t2[32:64], ALU.add, [B_t], [B_o])

    if doA:
        es = ExitStack()

        def sbl(name, shape, dt=F32, stack=None):
            uniq[0] += 1
            return (stack or es).enter_context(nc.sbuf_tensor("s_%s_%d" % (name, uniq[0]), list(shape), dt))

        wr = sbl("wr", [128, NCH, 128], BF16); wi = sbl("wi", [128, NCH, 128], BF16)
        B_gw = Buf("gatew")
        with nc.sbuf_tensor("s_wr_f", [128, NCH * 128], F32) as wr_f, nc.sbuf_tensor("s_wi_f", [128, NCH * 128], F32) as wi_f:
            P.dma("sp", lambda e: e.dma_start(out=wr_f[:], in_=wr_d), writes=[B_gw])
            P.dma("sp", lambda e: e.dma_start(out=wi_f[:], in_=wi_d), writes=[B_gw])
            cp("dve", wr[:].rearrange("p c k -> p (c k)"), wr_f[:], [B_gw], [B_gw])
            cp("dve", wi[:].rearrange("p c k -> p (c k)"), wi_f[:], [B_gw], [B_gw])
            P.barrier()
        cs = sbl("cs", [128, NCH]); nbr = sbl("nbr", [128, NCH]); nbi = sbl("nbi", [128, NCH])
        act(cs[:], svec["lam"][:], AF.Exp, [B_const], [B_const], scale=-1.0)
        act(cs[:], cs[:], AF.Ln, [B_const], [B_const], bias=1.0)
        ts("dve", cs[:], cs[:], -8.0, None, ALU.mult, None, [B_const], [B_const])
        ts("dve", nbr[:], svec["b_r"][:], -1.0, None, ALU.mult, None, [B_const], [B_const])
        ts("dve", nbi[:], svec["b_i"][:], -1.0, None, ALU.mult, None, [B_const], [B_const])
        state = sbl("state", [128, NCH]); B_state = Buf("state")
        xtail = sbl("xtail", [128, NCH, 3]); B_xtail = Buf("xtail")
        P.op("dve", lambda e: e.memset(state[:], 0.0), [], [B_state])
        P.op("dve", lambda e: e.memset(xtail[:], 0.0), [], [B_xtail])
        xnT = sbl("xnT", [128, KC, T], BF16); B_xnT = Buf("xnT")
        ytile = sbl("ytile", [128, 4, D]); B_y = [Buf("y%d" % i) for i in range(4)]
        NLT = 12
        lt_i = [0]

        def mixer_scope():
            m = ExitStack()
            o = {}
            o["mixT"] = sbl("mixT", [128, 16, T], BF16, m)
            o["gT"] = sbl("gT", [128, NCH, T], BF16, m)
            o["qmT"] = sbl("qmT", [128, 4, T], BF16, m)
            o["xbuf"] = [sbl("xbuf%d" % i, [128, 3 + T], F32, m) for i in range(4)]
            o["lt"] = [sbl("lt%d" % i, [128, T], F32, m) for i in range(NLT)]
            o["xcb"] = [sbl("xcb%d" % i, [128, T], BF16, m) for i in range(2)]
            o["B_mixT"] = [Buf("mixT%d" % i) for i in range(16)]
            o["B_gT"] = [Buf("gT%d" % i) for i in range(NCH)]
            o["B_qmT"] = Buf("qmT")
            o["B_xbuf"] = [Buf("xbuf%d" % i) for i in range(4)]
            o["B_lt"] = [Buf("lt%d" % i) for i in range(NLT)]
            o["B_xcb"] = [Buf("xcb%d" % i) for i in range(2)]
            return m, o

        def new_lt(o):
            i = lt_i[0] % NLT
            lt_i[0] += 1
            return o["lt"][i], o["B_lt"][i]

        def lru_chunk(o, c, bank, B_bank, full):
            xi = c % 4
            X, B_X = o["xbuf"][xi], o["B_xbuf"][xi]
            cp("dve", X[:, 0:3], xtail[:, c, :], [B_xtail], [B_X])
            act(X[:, 3:3 + T], bank[:], AF.Copy, [B_bank], [B_X])
            cp("dve", xtail[:, c, :], X[:, T:T + 3], [B_X], [B_xtail])
            yield
            cw = svec["conv_w"]
            xc, B_xc = new_lt(o)
            ts("dve", xc[:], X[:, 0:T], cw[:, c * 4:c * 4 + 1], svec["conv_b"][:, c:c + 1], ALU.mult, ALU.add, [B_X, B_const], [B_xc])
            for j in range(1, 4):
                stt(xc[:], X[:, j:j + T], cw[:, c * 4 + j:c * 4 + j + 1], xc[:], ALU.mult, ALU.add, [B_X, B_const, B_xc], [B_xc])
            yield
            xb_i = c % 2
            xcb, B_xcb = o["xcb"][xb_i], o["B_xcb"][xb_i]
            act(xcb[:], xc[:], AF.Copy, [B_xc], [B_xcb])
            yield
            r_ps, B_r = next_ps()
            i_ps, B_i = next_ps()
            mm(r_ps[:], wr[:, c, :], xcb[:], True, True, [B_gw, B_xcb], [B_r])
            mm(i_ps[:], wi[:, c, :], xcb[:], True, True, [B_gw, B_xcb], [B_i])
            yield
            r, B_rr = new_lt(o)
            ig, B_ig = new_lt(o)
            act(r[:], r_ps[:], AF.Sigmoid, [B_r, B_const], [B_rr], bias=svec["b_r"][:, c:c + 1])
            act(ig[:], i_ps[:], AF.Sigmoid, [B_i, B_const], [B_ig], bias=svec["b_i"][:, c:c + 1])
            yield
            a, B_a = new_lt(o)
            act(a[:], r[:], AF.Exp, [B_rr, B_const], [B_a], scale=cs[:, c:c + 1])
            tt("dve", ig[:], ig[:], xc[:], ALU.mult, [B_ig, B_xc], [B_ig])
            yield
            stt(r[:], a[:], -1.0, a[:], ALU.mult, ALU.mult, [B_a], [B_rr])
            yield
            act(r[:], r[:], AF.Ln, [B_rr], [B_rr], bias=1.0)
            act(r[:], r[:], AF.Exp, [B_rr], [B_rr], scale=0.5)
            yield
            tt("dve", ig[:], ig[:], r[:], ALU.mult, [B_ig, B_rr], [B_ig])
            hh, B_hh = new_lt(o)
            P.op("dve", lambda e: e.tensor_tensor_scan(out=hh[:], data0=a[:], data1=ig[:], initial=state[:, c:c + 1],
                                                       op0=ALU.mult, op1=ALU.add), [B_a, B_ig, B_state], [B_hh])
            cp("dve", state[:, c:c + 1], hh[:, T - 1:T], [B_hh], [B_state])
            if full:
                tt("dve", o["mixT"][:, c, :], hh[:], o["gT"][:, c, :], ALU.mult, [B_hh, o["B_gT"][c]], [o["B_mixT"][c]])

        def gelu_chunk(o, c, bank, B_bank):
            x, B_x = new_lt(o)
            act(x[:], bank[:], AF.Copy, [B_bank], [B_x])
            yield
            u, B_u = new_lt(o)
            tt("dve", u[:], x[:], x[:], ALU.mult, [B_x], [B_u])
            ts("dve", u[:], u[:], 0.044715, 1.0, ALU.mult, ALU.add, [B_u], [B_u])
            tt("dve", u[:], u[:], x[:], ALU.mult, [B_u, B_x], [B_u])
            yield
            act(u[:], u[:], AF.Sigmoid, [B_u], [B_u], scale=2.0 * math.sqrt(2.0 / math.pi))
            yield
            tt("dve", o["gT"][:, c, :], x[:], u[:], ALU.mult, [B_x, B_u], [o["B_gT"][c]])

        def interleave(gens, width=2):
            for i in range(0, len(gens), width):
                live = list(gens[i:i + width])
                while live:
                    nxt = []
                    for g_ in live:
                        try:
                            next(g_)
                            nxt.append(g_)
                        except StopIteration:
                            pass
                    live = nxt

        def group_then(n, g, fn, width=2):
            items = []
            fm_group(n, g, xnT, B_xnT, T, lambda j, bank, B_bank: items.append((j, bank, B_bank)))
            interleave([fn(j, bank, B_bank) for (j, bank, B_bank) in items], width)

        m, o = mixer_scope()
        stage("precast")
        for t in range(NT if dbg != "kvonly" else 0):
            norm_T(lambda st, t=t: x_prev[t * T + st * 128: t * T + (st + 1) * 128, :], lambda st: [B_in],
                   svec["gc_mix_pre%d" % L[0]], xnT, B_xnT, 4)
            stage("norm0", [("xnT", xnT[:], [B_xnT]), ("xn", xn[:], [B_xn]), ("hs1", hs[1][:], [B_hs[1]]), ("stat", stat[:], [B_stat]), ("ident", ident[:], [B_const])])
            for g in range(3):
                group_then("a_w_in", g, lambda j, bank, B_bank: lru_chunk(o, j, bank, B_bank, False))
            stage("prev0", [("state", state[:], [B_state]), ("xtail", xtail[:], [B_xtail])])
        ts("dve", state[:], state[:], flag[:, 0:1], None, ALU.mult, None, [B_state, B_const], [B_state])
        ts("dve", xtail[:].rearrange("p c k -> p (c k)"), xtail[:].rearrange("p c k -> p (c k)"), flag[:, 0:1], None,
           ALU.mult, None, [B_xtail, B_const], [B_xtail])
        stage("prev", [("state", state[:], [B_state]), ("xtail", xtail[:], [B_xtail])])
        precast(["w_ff1_0", "w_ff2_0", "kv_w_down"])
        P.barrier()
        m.close()
        if dbg != "kvonly":
            mem_kv()
        stage("memkv", [("mkT", mkT[:], [B_mkT]), ("mvt", mvt[:], [B_mvt])])
        for t in range(NT):
            rows_x = lambda st, t=t: x_own[t * T + st * 128: t * T + (st + 1) * 128, :]
            rows_h = lambda st, t=t: h1[t * T + st * 128: t * T + (st + 1) * 128, :]
            bufs_h = lambda st, t=t: [B_h1[t * 4 + st]]
            if dbg != "kvonly":
                m, o = mixer_scope()
                norm_T(rows_x, lambda st: [B_in], svec["gc_mix_pre%d" % L[0]], xnT, B_xnT, 4)
                for g in (3, 4, 5):
                    group_then("a_w_in", g, lambda j, bank, B_bank: gelu_chunk(o, j - 12, bank, B_bank), 4)
                fm_group("a_w_in", 6, xnT, B_xnT, T,
                         lambda j, bank, B_bank: act(o["qmT"][:, j - 24, :], bank[:], AF.Copy, [B_bank], [o["B_qmT"]]))
                mem_attend(o["qmT"], o["B_qmT"], lambda h: o["mixT"][:, 12 + h, :], lambda h: [o["B_mixT"][12 + h]])
                for g in range(3):
                    group_then("a_w_in", g, lambda j, bank, B_bank: lru_chunk(o, j, bank, B_bank, True))
                stage("mixer0", [("mixT", o["mixT"][:], o["B_mixT"]), ("gT", o["gT"][:], o["B_gT"]), ("qmT", o["qmT"][:], [o["B_qmT"]])])
                wo_ffn(es, m, o["mixT"], o["B_mixT"], xnT, B_xnT, ytile, B_y, rows_x, lambda st: [B_in], rows_h, bufs_h, rows_h, bufs_h)
                stage("ffn0")
            else:
                rows_h = rows_x
                bufs_h = lambda st: [B_in]
            with ExitStack() as ks:
                latn = sbl("latn", [128, KVL], BF16, ks); B_latn = Buf("latn")
                kpef = sbl("kpef", [128, ROPE], F32, ks); B_kpef = Buf("kpef")
                glat = sbl("glat", [128, KVL], F32, ks); B_glat = Buf("glat")
                latT = sbl("latT", [128, 4, T], BF16, ks); B_latT = Buf("latT")
                kpeT = sbl("kpeT", [64, T], BF16, ks); B_kpeT = Buf("kpeT")
                cosT = sbl("cosT", [64, T], F32, ks); sinT = sbl("sinT", [64, T], F32, ks); B_tab = Buf("tab")
                rt1 = sbl("rt1", [64, T], F32, ks); rt2 = sbl("rt2", [64, T], F32, ks); B_rt = Buf("rt")
                kx = sbl("kx", [64, T], F32, ks); B_kx = Buf("kx")
                rtmp = sbl("rtmp", [64, 2, T], F32, ks); rtmpi = sbl("rtmpi", [64, T], I32, ks); B_rtmp = Buf("rtmp")
                P.dma("act", lambda e, glat=glat: e.dma_start(out=glat[:], in_=vec["gr_kv_lat"].partition_broadcast(128)), reads=[B_in],
                      writes=[B_glat])
                norm_T(rows_h, bufs_h, svec["gc_kv_in"], xnT, B_xnT, 4)
                stage("kv_a", [("xnT", xnT[:], [B_xnT])])
                rope_tables(t * T, T, cosT, sinT, B_tab, rtmp, rtmpi, B_rtmp)
                stage("kv_b", [("cosT", cosT[:], [B_tab]), ("sinT", sinT[:], [B_tab])])
                b1 = [(ps[i], B_ps[i]) for i in range(4)]
                b2 = [(ps[4 + i], B_ps[4 + i]) for i in range(4)]
                for k0 in range(0, KC, 4):
                    w, B_w = piece("kv_w_down", 0, k0, 4)
                    for k in range(4):
                        kc = k0 + k
                        for st in range(4):
                            mm(b1[st][0][:, 0:KVL], xnT[:, kc, st * 128:(st + 1) * 128], w[:, k, 0:KVL], kc == 0, kc == KC - 1,
                               [B_w, B_xnT], [b1[st][1]])
                            mm(b2[st][0][:, 0:ROPE], xnT[:, kc, st * 128:(st + 1) * 128], w[:, k, KVL:KVL + ROPE], kc == 0,
                               kc == KC - 1, [B_w, B_xnT], [b2[st][1]])
                for st in range(4):
                    ss = next_stat()
                    act(junk[:, 0:KVL], b1[st][0][:, 0:KVL], AF.Square, [b1[st][1]], [B_stat], accum_out=ss)
                    rsqrt_ap(ss, ss, KVL, [B_stat], [B_stat])
                    stt(latn[:], b1[st][0][:, 0:KVL], ss, glat[:], ALU.mult, ALU.mult, [b1[st][1], B_stat, B_glat], [B_latn])
                    act(kpef[:], b2[st][0][:, 0:ROPE], AF.Copy, [b2[st][1]], [B_kpef])
                    pt, B_pt = b1[st]
                    ptb = pt[:].bitcast(BF16)
                    for j in range(4):
                        P.op("pe", lambda e, j=j, ptb=ptb, latn=latn: e.transpose(ptb[:, j * 128:(j + 1) * 128], latn[:, j * 128:(j + 1) * 128], ident[:]),
                             [B_latn, B_const], [B_pt])
                    for j in range(4):
                        cp("dve", latT[:, j, st * 128:(st + 1) * 128], ptb[:, j * 128:(j + 1) * 128], [B_pt], [B_latT])
                    pt2, B_pt2 = b2[st]
                    P.op("pe", lambda e, pt2=pt2, kpef=kpef: e.transpose(pt2[0:64, 0:128], kpef[:], ident_f[:]), [B_kpef, B_const], [B_pt2])
                    cp("dve", kx[:, st * 128:(st + 1) * 128], pt2[0:64, 0:128], [B_pt2], [B_kx])
                stage("kv_c", [("latT", latT[:], [B_latT]), ("kx", kx[:], [B_kx])])
                rope64(kx, B_kx, cosT, sinT, B_tab, kpeT, B_kpeT, rt1, rt2, B_rt)
                P.dma("act", lambda e, t=t, latT=latT: e.dma_start(
                    out=kvx_own[t, 0:512, :].rearrange("(c p) t -> p c t", p=128), in_=latT[:]), reads=[B_latT],
                      writes=[B_kvx[t]], kind="r")
                P.dma("act", lambda e, t=t, kpeT=kpeT: e.dma_start(out=kvx_own[t, 512:KVR, :], in_=kpeT[:]), reads=[B_kpeT],
                      writes=[B_kvx[t]], kind="r")
                if mode == "F":
                    P.dma("pool", lambda e, t=t: e.collective_compute("AllGather", ALU.bypass,
                                                                      replica_groups=[[0, 1], [2, 3], [4, 5], [6, 7]],
                                                                      ins=[kvx_own[t]], outs=[kv_all[t]]),
                          reads=[B_kvx[t]], writes=[B_kv_all[t]], sem_buf=B_kv_all[t], inc=1)
                    P.bg_sems.append(B_kv_all[t].wsem)
                P.barrier()
                stage("tile0")
            if mode == "F":
                g8 = [B_h1[t * 4 + 3]]
                if t == 0:
                    precast(["w_mem_k1", "w_mem_v1", "b_w_in", "b_w_qb", "kv_w_up"], g8)
                elif t == 1:
                    precast(["w_o1", "w_ff1_1"], g8)
                elif t == 2:
                    precast(["w_ff2_1"], g8)
        es.close()

    if mode == "F" and stop == "cc":
        with nc.sbuf_tensor("s_ccdump", [128, 9 * NT, T], BF16) as ccd:
            B_ccd = Buf("ccd")
            for n_ in range(NT):
                P.dma("sp", lambda e, n_=n_: e.dma_start(out=ccd[:, n_ * 9:(n_ + 1) * 9, :], in_=kv_all[n_].rearrange("(p k) c -> p k c", k=9)),
                      reads=B_kv_all, writes=[B_ccd])
            stage("cc", [("kvall", ccd[:], [B_ccd])])
    if doB:
        L[0] = 1
        es = ExitStack()

        def sbl(name, shape, dt=F32, stack=None):
            uniq[0] += 1
            return (stack or es).enter_context(nc.sbuf_tensor("s_%s_%d" % (name, uniq[0]), list(shape), dt))

        SC = 192.0 ** -0.5
        cq_scope = ExitStack()
        cqT = sbl("cqT", [128, 6, S_CORE], BF16, cq_scope); B_cqT = Buf("cqT")
        mem_kv()
        with ExitStack() as b1s:
            xnT = sbl("xnT", [128, KC, T], BF16, b1s); B_xnT = Buf("xnT")
            cq = sbl("cq", [128, 6, T], F32, b1s); B_cq = [Buf("cq%d" % i) for i in range(6)]
            sq = sbl("sq", [128, T], BF16, b1s); B_sq = Buf("sq")
            rb = sbl("rb", [128, T], F32, b1s); B_rb = Buf("rb")
            qmT = sbl("qmT", [128, 4, T], BF16, b1s); B_qmT = Buf("qmT")
            mo = sbl("mo", [128, 4, T], BF16, b1s); B_mo = [Buf("mo%d" % i) for i in range(4)]
            for t in range(NT):
                rows_h = lambda st, t=t: h1[t * T + st * 128: t * T + (st + 1) * 128, :]
                norm_T(rows_h, lambda st, t=t: [B_h1[t * 4 + st]], svec["gc_mix_pre%d" % L[0]], xnT, B_xnT, 4)
                for g in range(3):
                    fm_group("b_w_in", g, xnT, B_xnT, T,
                             lambda j, bank, B_bank: act(cq[:, j, :], bank[:], AF.Copy, [B_bank], [B_cq[j]]))
                s_ps, B_s = next_ps()
                for j in range(6):
                    tt("dve", sq[:], cq[:, j, :], cq[:, j, :], ALU.mult, [B_cq[j]], [B_sq])
                    mm(s_ps[:], ones_bf[:], sq[:], j == 0, j == 5, [B_sq, B_const], [B_s])
                ts("dve", rb[:], s_ps[:], 1.0 / QL, EPS, ALU.mult, ALU.add, [B_s], [B_rb])
                act(rb[:], rb[:], AF.Ln, [B_rb], [B_rb])
                act(rb[:], rb[:], AF.Exp, [B_rb], [B_rb], scale=-0.5)
                for j in range(6):
                    stt(cqT[:, j, t * T:(t + 1) * T], cq[:, j, :], svec["gc_qa"][:, j:j + 1], rb[:], ALU.mult, ALU.mult,
                        [B_cq[j], B_rb, B_const], [B_cqT])
                for g in (3, 4):
                    fm_group("b_w_in", g, xnT, B_xnT, T,
                             lambda j, bank, B_bank: act(qmT[:, j - 6, :], bank[:], AF.Copy, [B_bank], [B_qmT]))
                mem_attend(qmT, B_qmT, lambda h: mo[:, h, :], lambda h: [B_mo[h]])
                for h in range(4):
                    P.dma("act", lambda e, h=h, t=t: e.dma_start(out=mixd[12 + h, :, t * T:(t + 1) * T], in_=mo[:, h, :]),
                          reads=[B_mo[h]], writes=[B_mixd[12 + h]], kind="r")
            P.barrier()
        with ExitStack() as b3s:
            cos64 = sbl("cos64", [64, S_CORE], F32, b3s); sin64 = sbl("sin64", [64, S_CORE], F32, b3s); B_tab = Buf("tab")
            with ExitStack() as rs:
                rtmp = sbl("rtmp", [64, 2, T], F32, rs); rtmpi = sbl("rtmpi", [64, T], I32, rs); B_rtmp = Buf("rtmp")
                for t in range(NT):
                    rope_tables(t * T, T, cos64[:, t * T:(t + 1) * T], sin64[:, t * T:(t + 1) * T], B_tab, rtmp, rtmpi, B_rtmp)
                P.barrier()
            latA = sbl("latA", [128, 4, 2 * S_CORE], BF16, b3s); B_latA = Buf("latA")
            kpeA = sbl("kpeA", [64, 2 * S_CORE], BF16, b3s); B_kpeA = Buf("kpeA")
            for t in range(NT):
                P.dma("act", lambda e, t=t: e.dma_start(out=latA[:, :, t * T:(t + 1) * T],
                                                        in_=kv_all[t, 0:512, :].rearrange("(c p) t -> p c t", p=128)),
                      reads=[B_kv_all[t]], writes=[B_latA])
                P.dma("act", lambda e, t=t: e.dma_start(out=latA[:, :, S_CORE + t * T:S_CORE + (t + 1) * T],
                                                        in_=kvx_own[t, 0:512, :].rearrange("(c p) t -> p c t", p=128)),
                      reads=[B_kvx[t]], writes=[B_latA])
                P.dma("act", lambda e, t=t: e.dma_start(out=kpeA[:, t * T:(t + 1) * T], in_=kv_all[t, 512:KVR, :]),
                      reads=[B_kv_all[t]], writes=[B_kpeA])
                P.dma("act", lambda e, t=t: e.dma_start(out=kpeA[:, S_CORE + t * T:S_CORE + (t + 1) * T], in_=kvx_own[t, 512:KVR, :]),
                      reads=[B_kvx[t]], writes=[B_kpeA])
            kT = sbl("kT", [128, 2 * S_CORE], BF16, b3s); B_kT = Buf("kT")
            vt = sbl("vt", [128, 32, 128], BF16, b3s); B_vt = Buf("vt")
            qnT = sbl("qnT", [128, S_CORE], BF16, b3s); B_qnT = Buf("qnT")
            qrT = sbl("qrT", [64, S_CORE], BF16, b3s); B_qrT = Buf("qrT")
            oT = sbl("oT", [128, S_CORE], BF16, b3s); B_oT = Buf("oT")
            rt1 = sbl("rt1", [64, T], F32, b3s); rt2 = sbl("rt2", [64, T], F32, b3s); B_rt = Buf("rt")
            for h in range(NH):
                wu, B_wu = piece("kv_w_up", h, 0, 4)
                for blk in range(8):
                    bank, B_bank = next_ps()
                    for lc in range(4):
                        mm(bank[:], wu[:, lc, 0:128], latA[:, lc, blk * 512:(blk + 1) * 512], lc == 0, lc == 3,
                           [B_wu, B_latA], [B_bank])
                    act(kT[:, blk * 512:(blk + 1) * 512], bank[:], AF.Copy, [B_bank], [B_kT])
                for c4 in range(8):
                    bank, B_bank = next_ps()
                    for cc in range(4):
                        c = c4 * 4 + cc
                        for lc in range(4):
                            mm(bank[:, cc * 128:(cc + 1) * 128], latA[:, lc, c * 128:(c + 1) * 128], wu[:, lc, 128:256],
                               lc == 0, lc == 3, [B_wu, B_latA], [B_bank])
                    o_v = vt[:, c4 * 4:(c4 + 1) * 4, :].rearrange("p a b -> p (a b)")
                    if c4 < 4:
                        ts("dve", o_v, bank[:], flag[:, 0:1], None, ALU.mult, None, [B_bank, B_const], [B_vt])
                    else:
                        cp("dve", o_v, bank[:], [B_bank], [B_vt])
                wq, B_wq = piece("b_w_qb", h, 0, 6)
                for t in range(NT):
                    bn, B_bn = next_ps()
                    br, B_br = next_ps()
                    for kc in range(6):
                        mm(bn[:], wq[:, kc, 0:128], cqT[:, kc, t * T:(t + 1) * T], kc == 0, kc == 5, [B_wq, B_cqT], [B_bn])
                    for kc in range(6):
                        mm(br[0:64, :], wq[:, kc, 128:192], cqT[:, kc, t * T:(t + 1) * T], kc == 0, kc == 5, [B_wq, B_cqT], [B_br])
                    act(qnT[:, t * T:(t + 1) * T], bn[:], AF.Copy, [B_bn], [B_qnT])
                    rope64(br, B_br, cos64[:, t * T:(t + 1) * T], sin64[:, t * T:(t + 1) * T], B_tab,
                           qrT[:, t * T:(t + 1) * T], B_qrT, rt1, rt2, B_rt)
                for t in range(NT):
                    nch = 16 + 4 * (t + 1)

                    def col0(c, t=t):
                        return max(0, c - 16 - 4 * t) * 128
                    attend([qnT[:, t * T:(t + 1) * T], qrT[:, t * T:(t + 1) * T]],
                           [lambda c: kT[:, c * 128:(c + 1) * 128], lambda c: kpeA[:, c * 128:(c + 1) * 128]],
                           lambda c: vt[:, c, :], lambda c: (flagones[:] if c < 16 else ones_bf[:]), nch, col0, SC, zbias[:, 0:1],
                           oT[:, t * T:(t + 1) * T], [B_oT], [B_qnT, B_qrT], [B_kT, B_kpeA, B_vt],
                           diag_fn=lambda c, t=t: c >= 16 + 4 * t)
                P.dma("act", lambda e, h=h: e.dma_start(out=mixd[h], in_=oT[:]), reads=[B_oT], writes=[B_mixd[h]], kind="r")
            P.barrier()
        cq_scope.close()
        xnT = sbl("xnT2", [128, KC, T], BF16); B_xnT = Buf("xnT")
        ytile = sbl("ytile", [128, 4, D]); B_y = [Buf("y%d" % i) for i in range(4)]
        B_mixT = [Buf("mixT%d" % i) for i in range(16)]
        for t in range(NT):
            rows_h = lambda st, t=t: h1[t * T + st * 128: t * T + (st + 1) * 128, :]
            rows_m = lambda st, t=t: hmid[t * T + st * 128: t * T + (st + 1) * 128, :]
            rows_o = lambda st, t=t: out_d[t * T + st * 128: t * T + (st + 1) * 128, :]
            m = ExitStack()
            mixT = sbl("mixT", [128, 16, T], BF16, m)
            for kc in range(16):
                P.dma("act", lambda e, kc=kc, t=t, mixT=mixT: e.dma_start(out=mixT[:, kc, :], in_=mixd[kc, :, t * T:(t + 1) * T]),
                      reads=[B_mixd[kc]], writes=[B_mixT[kc]])
            wo_ffn(es, m, mixT, B_mixT, xnT, B_xnT, ytile, B_y, rows_h, lambda st, t=t: [B_h1[t * 4 + st]], rows_m,
                   lambda st, t=t: [B_hmid[t * 4 + st]], rows_o, lambda st, t=t: [B_outr[t * 4 + st]])
        es.close()

    P.barrier(final=True)
    P.emit()
    return nc


def _cols(v, n):
    return np.ascontiguousarray(np.asarray(v, np.float32).reshape(n, 128).T)


_NC_CACHE = {}


def _get_nc(mode):
    if mode not in _NC_CACHE:
        _NC_CACHE[mode] = build(mode)
    return _NC_CACHE[mode]


def kernel(x, mem, positions, g_mix_pre, g_mix_post, g_mlp_pre, g_mlp_post, g_mem, w_mem_k, w_mem_v, w_o, w_ff1, w_ff2,
           a_w_in, a_conv_w, a_conv_b, a_w_rgate, a_b_rgate, a_w_igate, a_b_igate, a_lambda,
           b_w_in, b_g_qa, b_w_qb, kv_g_in, kv_w_down, kv_g_latent, kv_w_up):
    f = lambda a: np.ascontiguousarray(np.asarray(a, dtype=np.float32))
    x = f(x); mem = f(mem); positions = np.asarray(positions, dtype=np.int32)
    ident = np.eye(128, dtype=np.float32)
    invf = (10000.0 ** (-np.arange(32, dtype=np.float32) / np.float32(32))).astype(np.float32)
    invf64 = np.ascontiguousarray(np.concatenate([invf, invf]).reshape(64, 1))
    ncore = 8
    common = []
    for c in range(ncore):
        b, hf = c // 2, c % 2
        pos = positions[b, hf * S_CORE:(hf + 1) * S_CORE]
        common.append({
            "ident": ident, "flag": np.full((128, 1), float(hf), np.float32), "invf": invf64,
            "pos64": np.ascontiguousarray(np.broadcast_to(pos[None, :], (64, S_CORE))).astype(np.int32),
            "mem": f(mem[b]),
        })

    def lvec(l):
        return {
            "gc_mix_pre%d" % l: _cols(g_mix_pre[l], KC), "gc_mlp_pre%d" % l: _cols(g_mlp_pre[l], KC), "gc_mem%d" % l: _cols(g_mem[l], KC),
            "gr_mix_post%d" % l: f(g_mix_post[l]).reshape(1, D), "gr_mlp_post%d" % l: f(g_mlp_post[l]).reshape(1, D),
        }

    wA = {"w_mem_k0": f(w_mem_k[0]), "w_mem_v0": f(w_mem_v[0]), "a_w_in": f(a_w_in[0]), "w_o0": f(w_o[0]),
          "w_ff1_0": f(w_ff1[0]), "w_ff2_0": f(w_ff2[0]), "kv_w_down": f(kv_w_down)}
    vA = lvec(0)
    vA.update({
        "conv_w": np.ascontiguousarray(f(a_conv_w[0]).reshape(4, NCH, 128).transpose(2, 1, 0).reshape(128, NCH * 4)),
        "conv_b": _cols(a_conv_b[0], NCH), "b_r": _cols(a_b_rgate[0], NCH), "b_i": _cols(a_b_igate[0], NCH),
        "lam": _cols(a_lambda[0], NCH), "gc_kv_in": _cols(kv_g_in, KC), "gr_kv_lat": f(kv_g_latent).reshape(1, KVL),
        "a_w_r": np.ascontiguousarray(f(a_w_rgate[0]).transpose(1, 0, 2).reshape(128, NCH * 128)),
        "a_w_i": np.ascontiguousarray(f(a_w_igate[0]).transpose(1, 0, 2).reshape(128, NCH * 128)),
    })
    wB = {"w_mem_k1": f(w_mem_k[1]), "w_mem_v1": f(w_mem_v[1]), "b_w_in": f(b_w_in[0]), "b_w_qb": f(b_w_qb[0]),
          "kv_w_up": f(kv_w_up), "w_o1": f(w_o[1]), "w_ff1_1": f(w_ff1[1]), "w_ff2_1": f(w_ff2[1])}
    vB = lvec(1)
    vB["gc_qa"] = _cols(b_g_qa[0], 6)
    zeros_prev = np.zeros((S_CORE, D), np.float32)
    in_maps = []
    for c in range(ncore):
        b, hf = c // 2, c % 2
        m = dict(common[c]); m.update(wA); m.update(vA); m.update(wB); m.update(vB)
        m["x_own"] = np.ascontiguousarray(x[b, hf * S_CORE:(hf + 1) * S_CORE])
        m["x_prev"] = np.ascontiguousarray(x[b, 0:S_CORE]) if hf == 1 else zeros_prev
        in_maps.append(m)
    res = run_bass_kernel_spmd(_get_nc("F"), in_maps, core_ids=list(range(ncore))).results
    out = np.empty((4, 2 * S_CORE, D), np.float32)
    for c in range(ncore):
        b, hf = c // 2, c % 2
        out[b, hf * S_CORE:(hf + 1) * S_CORE] = res[c]["out"]
    return out
```
